# Optimizing a Trainium2 kernel written in Bass

```python
import math
import jax, jax.numpy as jnp
from jax import lax
import numpy as np


D_MODEL = 2048
BATCH = 4
SEQ = 4096
DEPTH = 2
DEC_BATCH = 2
DEC_SEQ = 4096
PAST_LEN = 128

GRID_W = 64
NA_HEADS = 12
NA_HEAD_DIM = 64
NA_WIDTH = NA_HEADS * NA_HEAD_DIM
NA_WIN_ROWS_MAX = 8
NA_WIN_COLS = 16
RPB_ROWS = 2 * NA_WIN_ROWS_MAX - 1
RPB_COLS = 2 * NA_WIN_COLS - 1
CONV_WIDTH = 768
CONV_KERNEL = 31
MEM_TOKENS = 256
MEM_HEADS = 4
MEM_HEAD_DIM = 128
MEM_WIDTH = MEM_HEADS * MEM_HEAD_DIM
D_FF = -(-8 * D_MODEL // (3 * 256)) * 256
DEEPNORM_ALPHA = (2 * DEPTH) ** 0.25
DEEPNORM_BETA = (8 * DEPTH) ** -0.25
LN_EPS = 1e-5
IN_SPLITS = (
    NA_WIDTH,
    2 * NA_WIDTH,
    3 * NA_WIDTH,
    3 * NA_WIDTH + 2 * CONV_WIDTH,
    3 * NA_WIDTH + 2 * CONV_WIDTH + MEM_WIDTH,
    3 * NA_WIDTH + 2 * CONV_WIDTH + MEM_WIDTH + D_MODEL,
    3 * NA_WIDTH + 2 * CONV_WIDTH + MEM_WIDTH + 2 * D_MODEL,
)
IN_TOTAL = 3 * NA_WIDTH + 2 * CONV_WIDTH + MEM_WIDTH + 3 * D_MODEL

kernel_name = "hybrid_natten_conformer_memory_encoder"


def layer_norm(x, g, b):
    xf = x.astype(jnp.float32)
    mu = jnp.mean(xf, axis=-1, keepdims=True)
    var = jnp.mean(jnp.square(xf - mu), axis=-1, keepdims=True)
    y = (xf - mu) * lax.rsqrt(var + LN_EPS)
    return (y * g.astype(jnp.float32) + b.astype(jnp.float32)).astype(x.dtype)


def neighbourhood_attention(q, k, v, rpb):
    B, T, H, dh = q.shape
    rows = T // GRID_W
    kh = min(NA_WIN_ROWS_MAX, rows)
    kw = NA_WIN_COLS
    qg = (q * (dh ** -0.5)).reshape(B, rows, GRID_W, H, dh).transpose(1, 0, 2, 3, 4)
    kg = k.reshape(B, rows, GRID_W, H, dh)
    vg = v.reshape(B, rows, GRID_W, H, dh)
    cols = np.arange(GRID_W)
    col_start = np.clip(cols - kw // 2, 0, GRID_W - kw)
    col_idx = col_start[:, None] + np.arange(kw)[None, :]
    col_off = col_idx - cols[:, None] + (NA_WIN_COLS - 1)
    col_bias = rpb[:, :, col_off].astype(jnp.float32)

    def row_block(args):
        r, q_r = args
        start = jnp.clip(r - kh // 2, 0, rows - kh)
        k_rows = lax.dynamic_slice_in_dim(kg, start, kh, axis=1)
        v_rows = lax.dynamic_slice_in_dim(vg, start, kh, axis=1)
        k_win = k_rows[:, :, col_idx]
        v_win = v_rows[:, :, col_idx]
        row_off = start + jnp.arange(kh) - r + (NA_WIN_ROWS_MAX - 1)
        bias = jnp.take(col_bias, row_off, axis=1).transpose(0, 2, 1, 3)
        s = jnp.einsum('bqhd,biqjhd->bhqij', q_r, k_win).astype(jnp.float32) + bias[None]
        p = jax.nn.softmax(s.reshape(B, H, GRID_W, kh * kw), axis=-1)
        p = p.reshape(B, H, GRID_W, kh, kw).astype(v.dtype)
        return jnp.einsum('bhqij,biqjhd->bqhd', p, v_win)

    out = lax.map(row_block, (jnp.arange(rows), qg))
    return out.transpose(1, 0, 2, 3, 4).reshape(B, T, H * dh)


def conformer_conv(u_in, conv_w, conv_b, ln_g, ln_b):
    a, g = jnp.split(u_in, 2, axis=-1)
    u = a * jax.nn.sigmoid(g)
    pad = CONV_KERNEL // 2
    u = lax.conv_general_dilated(
        u, conv_w[:, None, :], window_strides=(1,), padding=((pad, pad),),
        dimension_numbers=('NWC', 'WIO', 'NWC'), feature_group_count=CONV_WIDTH) + conv_b
    return jax.nn.silu(layer_norm(u, ln_g, ln_b))


def memory_attention(q, mem_kv):
    B, T, _ = q.shape
    M = mem_kv.shape[1]
    qh = (q * (MEM_HEAD_DIM ** -0.5)).reshape(B, T, MEM_HEADS, MEM_HEAD_DIM)
    k, v = jnp.split(mem_kv, 2, axis=-1)
    k = k.reshape(B, M, MEM_HEADS, MEM_HEAD_DIM)
    v = v.reshape(B, M, MEM_HEADS, MEM_HEAD_DIM)
    s = jnp.einsum('bthd,bmhd->bhtm', qh, k).astype(jnp.float32)
    p = jax.nn.softmax(s, axis=-1).astype(v.dtype)
    return jnp.einsum('bhtm,bmhd->bthd', p, v).reshape(B, T, MEM_WIDTH)


def encoder_layer(x, mem, w_in, w_mem_kv, rpb, conv_w, conv_b, conv_ln_g, conv_ln_b,
                  w_pa, w_pb, w_pc, w_o, ln1_g, ln1_b, w_ffn_in, w_ffn_out, ln2_g, ln2_b):
    B, T, _ = x.shape
    h = x @ w_in
    q_na, k_na, v_na, u_conv, q_mem, g_na, g_conv, g_mem = jnp.split(h, IN_SPLITS, axis=-1)
    shp = (B, T, NA_HEADS, NA_HEAD_DIM)
    y_na = neighbourhood_attention(q_na.reshape(shp), k_na.reshape(shp), v_na.reshape(shp), rpb) @ w_pa
    y_conv = conformer_conv(u_conv, conv_w, conv_b, conv_ln_g, conv_ln_b) @ w_pb
    y_mem = memory_attention(q_mem, mem @ w_mem_kv) @ w_pc
    mixed = (jax.nn.sigmoid(g_na) * y_na + jax.nn.sigmoid(g_conv) * y_conv
             + jax.nn.sigmoid(g_mem) * y_mem)
    x = layer_norm(DEEPNORM_ALPHA * x + mixed @ w_o, ln1_g, ln1_b)
    gate, up = jnp.split(x @ w_ffn_in, 2, axis=-1)
    x = layer_norm(DEEPNORM_ALPHA * x + (jax.nn.silu(gate) * up) @ w_ffn_out, ln2_g, ln2_b)
    return x


def run_trunk(x, mem, params):
    for l in range(DEPTH):
        x = encoder_layer(x, mem, *[p[l] for p in params])
    return x


def setup_inputs(seed: int = 0) -> dict:
    key = jax.random.key(seed)
    ks = jax.random.split(key, 24)
    f32 = jnp.float32
    nrm = lambda k, shape, s: jax.random.normal(k, shape, f32) * s
    return {
        "x_prompt": nrm(ks[0], (BATCH, SEQ, D_MODEL), 1.0),
        "x_sample": nrm(ks[1], (DEC_BATCH, DEC_SEQ, D_MODEL), 1.0),
        "mem_prompt": nrm(ks[2], (BATCH, MEM_TOKENS, D_MODEL), 1.0),
        "mem_sample": nrm(ks[3], (DEC_BATCH, MEM_TOKENS, D_MODEL), 1.0),
        "w_in": nrm(ks[4], (DEPTH, D_MODEL, IN_TOTAL), D_MODEL ** -0.5),
        "w_mem_kv": nrm(ks[5], (DEPTH, D_MODEL, 2 * MEM_WIDTH), D_MODEL ** -0.5),
        "rpb": nrm(ks[6], (DEPTH, NA_HEADS, RPB_ROWS, RPB_COLS), 0.1),
        "conv_w": nrm(ks[7], (DEPTH, CONV_KERNEL, CONV_WIDTH), CONV_KERNEL ** -0.5),
        "conv_b": nrm(ks[8], (DEPTH, CONV_WIDTH), 0.01),
        "conv_ln_g": 1.0 + nrm(ks[9], (DEPTH, CONV_WIDTH), 0.01),
        "conv_ln_b": nrm(ks[10], (DEPTH, CONV_WIDTH), 0.01),
        "w_pa": nrm(ks[11], (DEPTH, NA_WIDTH, D_MODEL), NA_WIDTH ** -0.5),
        "w_pb": nrm(ks[12], (DEPTH, CONV_WIDTH, D_MODEL), CONV_WIDTH ** -0.5),
        "w_pc": nrm(ks[13], (DEPTH, MEM_WIDTH, D_MODEL), MEM_WIDTH ** -0.5),
        "w_o": nrm(ks[14], (DEPTH, D_MODEL, D_MODEL), D_MODEL ** -0.5 * DEEPNORM_BETA),
        "ln1_g": 1.0 + nrm(ks[15], (DEPTH, D_MODEL), 0.01),
        "ln1_b": nrm(ks[16], (DEPTH, D_MODEL), 0.01),
        "w_ffn_in": nrm(ks[17], (DEPTH, D_MODEL, 2 * D_FF), D_MODEL ** -0.5),
        "w_ffn_out": nrm(ks[18], (DEPTH, D_FF, D_MODEL), D_FF ** -0.5 * DEEPNORM_BETA),
        "ln2_g": 1.0 + nrm(ks[19], (DEPTH, D_MODEL), 0.01),
        "ln2_b": nrm(ks[20], (DEPTH, D_MODEL), 0.01),
    }


def reference(x_prompt, x_sample, mem_prompt, mem_sample, w_in, w_mem_kv, rpb, conv_w, conv_b,
              conv_ln_g, conv_ln_b, w_pa, w_pb, w_pc, w_o, ln1_g, ln1_b, w_ffn_in, w_ffn_out,
              ln2_g, ln2_b):
    params = (w_in, w_mem_kv, rpb, conv_w, conv_b, conv_ln_g, conv_ln_b, w_pa, w_pb, w_pc, w_o,
              ln1_g, ln1_b, w_ffn_in, w_ffn_out, ln2_g, ln2_b)
    y_prompt = run_trunk(x_prompt, mem_prompt, params)
    y_sample = run_trunk(x_sample, mem_sample, params)
    return (y_prompt, y_sample)
```

```python
from contextlib import ExitStack
import numpy as np
import concourse.bass as bass
import concourse.mybir as mybir
from concourse.bass_utils import run_bass_kernel_spmd

F32 = mybir.dt.float32
BF16 = mybir.dt.bfloat16
AF = mybir.ActivationFunctionType
ALU = mybir.AluOpType
NEG = -30000.0
LN_EPS = 1e-5
TT = 512
POOL_CONV = False
GW = 64


class Cfg:
    def __init__(self, D=2048, T=4096, H=12, CW=768, MH=4, M=256, DFF=5632, L=2):
        self.D, self.T, self.H, self.CW, self.MH, self.M, self.DFF, self.L = D, T, H, CW, MH, M, DFF, L
        self.KD = D // 128
        self.NAW = H * 64
        self.NAC = self.NAW // 128
        self.CC = CW // 128
        self.MW = MH * 128
        self.HC = DFF // 128
        self.ROWS = T // GW
        self.NT = T // TT
        self.NCH = T // 128
        self.oq = 0
        self.ok = self.NAW
        self.ov = 2 * self.NAW
        self.oa = 3 * self.NAW
        self.og = 3 * self.NAW + CW
        self.oqm = 3 * self.NAW + 2 * CW
        self.ogn = self.oqm + self.MW
        self.ogc = self.ogn + D
        self.ogm = self.ogc + D
        self.IN_TOTAL = self.ogm + D
        self.alpha = float((2 * L) ** 0.25)
        o = 0
        self.v_cw = o; o += L * self.CC * 31
        self.v_cb = o; o += L * self.CC
        self.v_cg = o; o += L * self.CC
        self.v_cbb = o; o += L * self.CC
        self.v_l1g = o; o += L * self.KD
        self.v_l1b = o; o += L * self.KD
        self.v_l2g = o; o += L * self.KD
        self.v_l2b = o; o += L * self.KD
        self.NV = o


def na_meta(ROWS):
    npairs = ROWS // 2

    def start_r(r):
        return min(max(r - 4, 0), ROWS - 8)

    cfg_keys = {}
    pair_cfg = []
    entries = []
    cfg_entries = []
    for i in range(npairs):
        ms = sorted({kr // 2 for qr in (0, 1) for kr in range(start_r(2 * i + qr), start_r(2 * i + qr) + 8)})
        key = []
        for m in ms:
            pat = tuple(
                tuple(start_r(2 * i + qr) <= 2 * m + kr < start_r(2 * i + qr) + 8 for kr in (0, 1)) for qr in (0, 1))
            key.append((m - i, pat))
        key = tuple(key)
        if key not in cfg_keys:
            cfg_keys[key] = len(cfg_entries)
            lst = []
            for (d, pat) in key:
                lst.append((d, len(entries)))
                entries.append((d, pat))
            cfg_entries.append(lst)
        pair_cfg.append(cfg_keys[key])
    counts = [pair_cfg.count(c) for c in range(len(cfg_entries))]
    main_cfg = int(np.argmax(counts))
    return pair_cfg, cfg_entries, entries, main_cfg


def na_index(entries):
    E = len(entries)
    idx = np.full((E, 128, 128), 465, dtype=np.int64)
    qc = np.arange(64)[:, None]
    kc = np.arange(64)[None, :]
    sc = np.clip(qc - 8, 0, 48)
    vc = (kc >= sc) & (kc < sc + 16)
    coff = kc - qc + 15
    for e, (d, pat) in enumerate(entries):
        for qr in (0, 1):
            for kr in (0, 1):
                if not pat[qr][kr]:
                    continue
                dr = 2 * d + kr - qr
                blk = np.where(vc, (dr + 7) * 31 + coff, 465)
                idx[e, qr * 64:(qr + 1) * 64, kr * 64:(kr + 1) * 64] = blk
    return idx


class Buf:
    __slots__ = ("name", "w", "w_eng", "r")

    def __init__(self, name):
        self.name = name
        self.w = None
        self.w_eng = None
        self.r = {}


class TV:
    __slots__ = ("ap", "bufs")

    def __init__(self, ap, bufs):
        self.ap = ap
        self.bufs = bufs if isinstance(bufs, (list, tuple)) else [bufs]

    def __getitem__(self, key):
        return TV(self.ap[key], self.bufs)

    def v(self, ap):
        return TV(ap, self.bufs)


ENGS = ("pe", "act", "dve", "pool", "sp")


class Sched:
    def __init__(self, nc, es):
        self.nc = nc
        self.es = es
        self.q = {e: [] for e in ENGS}
        self.sem = {}
        self.cnt = {}
        self.nsem = 0
        self.waited = {e: {} for e in ENGS}
        self.pending_unmarked = {e: False for e in ENGS}
        for e in ("pe", "act", "dve", "pool"):
            self._new_sem(e)

    def alloc_sem(self, name):
        self.nsem += 1
        return self.es.enter_context(self.nc.semaphore(f"{name}_{self.nsem}"))

    def _new_sem(self, e):
        self.sem[e] = self.alloc_sem("c" + e)
        self.cnt[e] = 0

    def new_epoch(self):
        for e in ("pe", "act", "dve", "pool"):
            assert not self.pending_unmarked[e], e
            if self.cnt[e] > 36000:
                self._new_sem(e)

    def _waits(self, eng, reads, writes):
        ws = []
        for t in reads:
            for b in t.bufs:
                if b.w is not None:
                    ws.append(b.w)
        for t in writes:
            for b in t.bufs:
                for (e2, tok) in b.r.values():
                    if e2 != eng or eng != "pe":
                        ws.append(tok)
                if b.w is not None and (b.w_eng != eng or eng != "pe"):
                    ws.append(b.w)
        out = []
        wd = self.waited[eng]
        for (sem, val) in ws:
            k = id(sem)
            if wd.get(k, (None, 0))[1] >= val:
                continue
            wd[k] = (sem, val)
            out.append((sem, val))
        best = {}
        for (sem, val) in out:
            k = id(sem)
            if k not in best or best[k][1] < val:
                best[k] = (sem, val)
        return list(best.values())

    def _update(self, eng, tok, reads, writes):
        for t in reads:
            for b in t.bufs:
                b.r[eng if eng != "dma" else ("dma", id(tok[0]))] = (eng, tok)
        for t in writes:
            for b in t.bufs:
                b.w = tok
                b.w_eng = eng
                b.r = {}

    def op(self, eng, fn, reads=(), writes=(), mark=True):
        ws = self._waits(eng, reads, writes)
        if mark:
            self.cnt[eng] += 1
            tok = (self.sem[eng], self.cnt[eng])
            self.pending_unmarked[eng] = False
            inc = (self.sem[eng], 1)
        else:
            tok = (self.sem[eng], self.cnt[eng] + 1)
            self.pending_unmarked[eng] = True
            inc = None
        self.q[eng].append((ws, fn, inc))
        self._update(eng, tok, reads, writes)
        return tok

    def dma(self, qeng, out, in_, sem_holder, out_tracked=True, in_tracked=True):
        reads = [in_] if in_tracked else []
        writes = [out] if out_tracked else []
        ws = self._waits(qeng, reads, writes)
        if sem_holder["cnt"] > 60000:
            sem_holder["sem"] = self.alloc_sem("dr")
            sem_holder["cnt"] = 0
        sem_holder["cnt"] += 16
        tok = (sem_holder["sem"], sem_holder["cnt"])
        o_ap, i_ap = out.ap, in_.ap
        self.q[qeng].append((ws, (lambda e, o=o_ap, i=i_ap: e.dma_start(out=o, in_=i)), (sem_holder["sem"], 16)))
        self._update("dma", tok, reads, writes)
        return tok

    def stream(self, name):
        return {"sem": self.alloc_sem("d" + name), "cnt": 0}

    def wait_only(self, eng, toks):
        ws = []
        wd = self.waited[eng]
        for (sem, val) in toks:
            k = id(sem)
            if wd.get(k, (None, 0))[1] >= val:
                continue
            wd[k] = (sem, val)
            ws.append((sem, val))
        if ws:
            self.q[eng].append((ws, None, None))

    def replay(self, eng_name, e):
        for (ws, fn, inc) in self.q[eng_name]:
            for (sem, val) in ws:
                e.wait_ge(sem, val)
            if fn is None:
                continue
            ins = fn(e)
            if inc is not None:
                ins.then_inc(inc[0], inc[1])


def build(cfg, pair_cfg, cfg_entries, entries, main_cfg):
    c = cfg
    D, T, H, KD, NAC, CC, MH, HC, L, NT = c.D, c.T, c.H, c.KD, c.NAC, c.CC, c.MH, c.HC, c.L, c.NT
    E = len(entries)
    nc = bass.Bass("TRN2", target_bir_lowering=False)

    def din(name, shape):
        return nc.dram_tensor(name, list(shape), F32, kind="ExternalInput").ap()

    xT_d = din("xT", [KD, 128, T])
    memT_d = din("memT", [KD, 128, c.M])
    w_in_d = din("w_in", [L, D, c.IN_TOTAL])
    w_mkv_d = din("w_mem_kv", [L, D, 2 * c.MW])
    w_pa_d = din("w_pa", [L, c.NAW, D])
    w_pb_d = din("w_pb", [L, c.CW, D])
    w_pc_d = din("w_pc", [L, c.MW, D])
    w_o_d = din("w_o", [L, D, D])
    w_fi_d = din("w_ffn_in", [L, D, 2 * c.DFF])
    w_fo_d = din("w_ffn_out", [L, c.DFF, D])
    tab_d = din("natab", [L, E, H, 128, 128])
    vecs_d = din("vecs", [128, c.NV])
    ident_d = din("ident", [128, 128])
    yT_d = nc.dram_tensor("yT", [KD, 128, T], F32, kind="ExternalOutput").ap()
    x1T_d = nc.dram_tensor("x1T", [KD, 128, T], F32, kind="Internal").ap()
    KT_d = nc.dram_tensor("KTs", [NAC, 128, T], BF16, kind="Internal").ap()
    V_d = nc.dram_tensor("Vs", [c.NCH, 128, H * 65], BF16, kind="Internal").ap()
    UT_d = nc.dram_tensor("UTs", [CC, 128, T + 32], BF16, kind="Internal").ap()
    WTOT = (D * c.IN_TOTAL + (c.NAW + c.CW + c.MW) * D + D * D + D * 2 * c.DFF + c.DFF * D) // 128
    wscr_d = [nc.dram_tensor(f"wscr{l_}", [128, WTOT], BF16, kind="Internal").ap() for l_ in range(L)]

    es = ExitStack()
    with es:
        S = Sched(nc, es)

        def sb(name, shape, dt):
            return es.enter_context(nc.sbuf_tensor("s_" + name, list(shape), dt))

        xres_t = sb("xres", [128, KD, TT], F32)
        xTb_t = sb("xTb", [128, KD, TT], BF16)
        NSLOT = 4
        SLOT = 8 * 512
        ring_t = [sb(f"ring{i}", [128, SLOT], BF16) for i in range(NSLOT)]
        ncfgI = len(cfg_entries[main_cfg])
        tabI_t = sb("tabI", [128, ncfgI * H, 128], BF16)
        memK_t = sb("memK", [128, MH, c.M], BF16)
        memV_t = sb("memV", [128, c.M // 128, c.MW], BF16)
        vecs_t = sb("vecs", [128, c.NV], F32)
        ident_t = sb("ident", [128, 128], BF16)
        ones_t = sb("ones", [128, 128], BF16)
        stat_t = sb("stat", [128, 4, TT], F32)
        cvr_t = sb("cvr", [128, CC, TT], F32)
        B_cvr = [Buf(f"cvr{i}") for i in range(CC)]
        rec_t = sb("rec", [128, 16], F32)

        A_KTw = NAC * 1024
        A_Vw = 8 * H * 65
        A_Uw = max(CC * 544, KD * TT + 8 * TT - A_KTw - A_Vw, 2 * CC * TT - A_KTw - A_Vw)
        A_q = H * TT
        A_qm = MH * TT
        A_na = NAC * TT
        A_cv = CC * TT
        A_mo = MH * TT
        maxE = max([len(v) for i_, v in enumerate(cfg_entries) if i_ != main_cfg] + [1])
        A_tabE = maxE * H * 128
        A_pT = 2 * 768
        A_naTok = c.NAW
        A_pTm = (c.M // 128) * TT
        A_sg = 2 * 2 * TT
        off = {}
        o = 0
        for nm, sz in (("KTw", A_KTw), ("Vw", A_Vw), ("Uw", A_Uw), ("q", A_q), ("qm", A_qm), ("mo", A_mo),
                       ("cv", A_cv), ("na", A_na), ("pT", A_pT), ("naTok", A_naTok), ("pTm", A_pTm)):
            o = (o + 1) & ~1
            off[nm] = (o, sz)
            o += sz
        span_lo = off["q"][0]
        A_h = HC * TT
        A_ln = 2 * KD * TT
        need = max(o, span_lo + A_h, span_lo + A_q + A_ln, span_lo + (NAC + CC) * TT + 4 * H * 65)
        assert off["qm"][1] + off["mo"][1] + off["cv"][1] >= A_tabE, "tabE alias"
        assert A_KTw >= 2 * CC * TT or True
        arena_t = sb("arena", [128, need], BF16)
        cells = {nm: Buf(nm) for nm in off}
        tail_buf = Buf("tail")
        cell_order = list(off.keys())

        def bufs_for(lo, hi):
            bl = []
            for nm in cell_order:
                a, sz = off[nm]
                if a < hi and a + sz > lo:
                    bl.append(cells[nm])
            if hi > o:
                bl.append(tail_buf)
            return bl

        def aview(lo, n, shape=None, dt=BF16):
            ap = arena_t[:, lo:lo + n]
            if dt == F32:
                ap = ap.bitcast(F32)
            if shape is not None:
                names = " ".join(f"a{i}" for i in range(len(shape)))
                kw = {f"a{i}": s for i, s in enumerate(shape)}
                ap = ap.rearrange(f"p ({names}) -> p {names}", **kw)
            return TV(ap, bufs_for(lo, lo + n))

        KTw = aview(off["KTw"][0], A_KTw, [NAC, 1024])
        Vw = aview(off["Vw"][0], A_Vw, [8, H * 65])
        Uw = aview(off["Uw"][0], CC * 544, [CC, 544])
        qpad = aview(off["q"][0], A_q, [H, TT])
        qmT = aview(off["qm"][0], A_qm, [MH, TT])
        moT = aview(off["mo"][0], A_mo, [MH, TT])
        cvT = aview(off["cv"][0], A_cv, [CC, TT])
        naT = aview(off["na"][0], A_na, [NAC, TT])
        pT = [aview(off["pT"][0] + i * 768, 768) for i in range(2)]
        pT[0].bufs = [Buf("pT0")]
        pT[1].bufs = [Buf("pT1")]
        pT_cells = bufs_for(off["pT"][0], off["pT"][0] + A_pT)
        naTok = aview(off["naTok"][0], A_naTok)
        pTm = aview(off["pTm"][0], A_pTm, [c.M // 128, TT])
        sg_t = sb("sgt", [128, 2, TT], F32)
        sg = [TV(sg_t[:, i, :], Buf(f"sg{i}")) for i in range(2)]
        tabE = aview(off["qm"][0], A_tabE, [maxE * H, 128])
        mixT = aview(off["KTw"][0], KD * TT, [KD, TT])
        mixacc = aview(off["KTw"][0] + KD * TT, 4 * 2 * TT, [4, TT], F32)
        assert KD * TT + 8 * TT <= A_KTw + A_Vw + A_Uw
        cvb = aview(off["KTw"][0], CC * TT, [CC, TT])
        cvsq = aview(off["KTw"][0] + CC * TT, CC * TT, [CC, TT])
        assert 2 * CC * TT <= A_KTw + A_Vw
        cvr = aview(off["q"][0], 2 * CC * TT, [CC, TT], F32)
        assert 2 * CC * TT <= A_q + A_qm + A_mo or 2 * CC * TT <= A_q
        hT = aview(span_lo, A_h, [HC, TT])
        rb = aview(span_lo + A_q, KD * TT, [KD, TT])
        rsq = aview(span_lo + A_q + KD * TT, KD * TT, [KD, TT])
        kst = aview(span_lo, NAC * TT, [NAC, TT])
        ust = aview(span_lo + NAC * TT, CC * TT, [CC, TT])
        vst = aview(span_lo + (NAC + CC) * TT, 4 * H * 65, [4, H, 65])
        extra_span = [pT[0].bufs[0], pT[1].bufs[0]]
        for tv in (hT, rb, rsq, kst, ust, vst):
            tv.bufs = list(tv.bufs) + extra_span

        banks = []
        for i in range(8):
            t_ = es.enter_context(nc.psum_tensor(f"ps{i}", [128, 512], F32))
            banks.append(TV(t_[:, :], Buf(f"bank{i}")))
        bank_rr = [0]

        def next_bank():
            b = banks[bank_rr[0] % 8]
            bank_rr[0] += 1
            return b

        B_xres = [Buf(f"xres{i}") for i in range(KD)]
        xres = [TV(xres_t[:, i, :], B_xres[i]) for i in range(KD)]
        xres_all = TV(xres_t[:, :, :], B_xres)
        B_xTb = [Buf(f"xTb{i}") for i in range(KD)]
        xTb = TV(xTb_t[:, :, :], B_xTb)
        xTbc_ = [TV(xTb_t[:, i, :], B_xTb[i]) for i in range(KD)]
        ring = [TV(ring_t[i][:, :], Buf(f"ring{i}")) for i in range(NSLOT)]
        ring_st = [S.stream(f"ring{i}") for i in range(NSLOT)]
        ring_i = [0]
        wb_st = [S.stream(f"wb{i}") for i in range(NSLOT)]
        wcache = {}
        wcache_off = [0] * L
        wctx = {"l": 0, "phase": None, "t": 0, "idx": 0}
        tabI = TV(tabI_t[:, :, :], Buf("tabI"))
        memK = TV(memK_t[:, :, :], Buf("memK"))
        memV = TV(memV_t[:, :, :], Buf("memV"))
        vecs = TV(vecs_t[:, :], Buf("vecs"))
        ident = TV(ident_t[:, :], Buf("ident"))
        ones = TV(ones_t[:, :], Buf("ones"))
        stat = [TV(stat_t[:, i, :], Buf(f"stat{i}")) for i in range(4)]
        rec = TV(rec_t[:, :], Buf("rec"))

        B_x1 = [Buf(f"x1d{t}") for t in range(NT)]
        B_KT = [Buf(f"ktd{t}") for t in range(NT)]
        B_V = [Buf(f"vd{t}") for t in range(NT)]
        B_U = [Buf(f"ud{t}") for t in range(NT)]
        B_Upad = Buf("upad")

        st_misc = S.stream("misc")
        st_xres = S.stream("xres")
        st_xTb = S.stream("xTb")
        st_xTb2 = S.stream("xTb2")
        st_xq = [S.stream(f"xq{i}") for i in range(4)]
        st_kw = S.stream("kw")
        st_vw = S.stream("vw")
        st_uw = S.stream("uw")
        st_ks = S.stream("ks")
        st_us = S.stream("us")
        st_vs = S.stream("vs")
        st_out = S.stream("out")
        st_tabI = S.stream("tabI")
        st_tabE = S.stream("tabE")
        st_mem = S.stream("mem")

        def mm(out, lhsT, rhs, start, stop, mark):
            o_, l_, r_ = out.ap, lhsT.ap, rhs.ap
            return S.op("pe", lambda e: e.matmul(o_, l_, r_, start=start, stop=stop),
                        reads=[lhsT, rhs], writes=[out], mark=mark)

        def act(out, in_, func, bias=None, scale=None, extra_reads=()):
            o_, i_ = out.ap, in_.ap
            kw = {}
            if bias is not None:
                kw["bias"] = bias.ap if isinstance(bias, TV) else bias
            if scale is not None:
                kw["scale"] = scale.ap if isinstance(scale, TV) else scale
            rd = [in_] + [x for x in (bias, scale) if isinstance(x, TV)] + list(extra_reads)
            return S.op("act", lambda e: e.activation(o_, i_, func, **kw), reads=rd, writes=[out])

        def tt(out, in0, in1, op):
            o_, a_, b_ = out.ap, in0.ap, in1.ap
            return S.op("dve", lambda e: e.tensor_tensor(o_, a_, b_, op), reads=[in0, in1], writes=[out])

        def ts(out, in0, s1, s2, op0, op1=None, eng="dve"):
            o_, a_ = out.ap, in0.ap
            s1_ = s1.ap if isinstance(s1, TV) else s1
            s2_ = s2.ap if isinstance(s2, TV) else s2
            rd = [in0] + [x for x in (s1, s2) if isinstance(x, TV)]
            if op1 is None:
                return S.op(eng, lambda e: e.tensor_scalar(o_, a_, s1_, None, op0), reads=rd, writes=[out])
            return S.op(eng, lambda e: e.tensor_scalar(o_, a_, s1_, s2_, op0, op1), reads=rd, writes=[out])

        def stt(out, in0, scalar, in1, op0, op1, eng="dve"):
            o_, a_, b_ = out.ap, in0.ap, in1.ap
            s_ = scalar.ap if isinstance(scalar, TV) else scalar
            rd = [in0, in1] + ([scalar] if isinstance(scalar, TV) else [])
            return S.op(eng, lambda e: e.scalar_tensor_tensor(o_, a_, s_, b_, op0, op1), reads=rd, writes=[out])

        def recip(out, in_):
            o_, i_ = out.ap, in_.ap
            return S.op("dve", lambda e: e.reciprocal(o_, i_), reads=[in_], writes=[out])

        def memset(out, val, eng="dve"):
            o_ = out.ap
            return S.op(eng, lambda e: e.memset(o_, val), reads=[], writes=[out])

        def dram(ap):
            return TV(ap, [])

        def wpiece(w2d, r0, nk, c0, ncols):
            s = ring_i[0] % NSLOT
            ring_i[0] += 1
            assert nk * ncols <= SLOT
            n = nk * ncols
            flat = ring[s].v(ring[s].ap[:, 0:n])
            dst = ring[s].v(ring[s].ap[:, 0:n].rearrange("p (k n) -> p k n", k=nk))
            src = w2d[r0:r0 + nk * 128, c0:c0 + ncols].rearrange("(k p) n -> p k n", p=128)
            if wctx["phase"] is None:
                S.dma("pool", dst, dram(src), ring_st[s])
                return dst
            l_ = wctx["l"]
            key = (l_, wctx["phase"], wctx["idx"])
            wctx["idx"] += 1
            sig = (r0, nk, c0, ncols)
            if wctx["t"] == 0:
                o_ = wcache_off[l_]
                wcache_off[l_] += n
                assert wcache_off[l_] <= WTOT
                cb = Buf("wc")
                wcache[key] = (o_, sig, cb)
                S.dma("pool", dst, dram(src), ring_st[s])
                S.dma("sp", TV(wscr_d[l_][:, o_:o_ + n], [cb]), flat, wb_st[s])
            else:
                o_, sig0, cb = wcache[key]
                assert sig0 == sig, (key, sig0, sig)
                S.dma("pool", flat, TV(wscr_d[l_][:, o_:o_ + n], [cb]), ring_st[s])
            return dst

        def split_k(K, kp):
            out_ = []
            k0 = 0
            while k0 < K:
                n = min(kp, K - k0)
                out_.append((k0, n))
                k0 += n
            return out_

        def proj_group(specs, evac):
            allb = []
            for sp in specs:
                w2d, K, c0, ncols, rhs_fn = sp["w"], sp["K"], sp["c0"], sp["ncols"], sp["rhs"]
                kp = sp.get("kp", 8)
                r0 = sp.get("r0", 0)
                nn = sp.get("n", TT)
                nfc = ncols // 128
                bks = [next_bank() for _ in range(nfc)]
                pieces = split_k(K, kp)
                for gi, (k0, nk) in enumerate(pieces):
                    pc = wpiece(w2d, r0 + k0 * 128, nk, c0, ncols)
                    for fc in range(nfc):
                        for k in range(nk):
                            last = (gi == len(pieces) - 1 and k == nk - 1)
                            mm(bks[fc][:, 0:nn], pc[:, k, fc * 128:(fc + 1) * 128], rhs_fn(k0 + k),
                               start=(gi == 0 and k == 0), stop=last,
                               mark=last or (fc == nfc - 1 and k == nk - 1))
                allb.append(bks)
            evac(allb)

        def proj_tm(w2d, K, c0, ncols, lhs_fn, njt, evac, kp=8):
            bks = [next_bank() for _ in range(njt)]
            pieces = split_k(K, kp)
            for gi, (k0, nk) in enumerate(pieces):
                pc = wpiece(w2d, k0 * 128, nk, c0, ncols)
                for j in range(njt):
                    for k in range(nk):
                        last = (gi == len(pieces) - 1 and k == nk - 1)
                        mm(bks[j][:, 0:ncols], lhs_fn(k0 + k, j), pc[:, k, :],
                           start=(gi == 0 and k == 0), stop=last, mark=last or (j == njt - 1 and k == nk - 1))
            evac(bks)

        def col_groups(c0, n, g=512):
            out_ = []
            a = 0
            while a < n:
                m = min(g, n - a)
                out_.append((c0 + a, m))
                a += m
            return out_

        def vcol(o_, n=1):
            return vecs[:, o_:o_ + n]

        xk = lambda k: xTb[:, k, :]

        def ln_part1(src_chunks, nchunks):
            for c_ in range(nchunks):
                act(rb[:, c_, :], src_chunks[c_], AF.Copy)
                tt(rsq[:, c_, :], src_chunks[c_], src_chunks[c_], ALU.mult)

        def layer_norm_fm(src_chunks, nchunks, width, g_off, b_off, outs, skip_part1=False, stats_only=False):
            if not skip_part1:
                ln_part1(src_chunks, nchunks)
            b1 = next_bank()
            b2 = next_bank()
            for c_ in range(nchunks):
                mm(b1, ones, rb[:, c_, :], start=(c_ == 0), stop=(c_ == nchunks - 1), mark=(c_ == nchunks - 1))
            for c_ in range(nchunks):
                mm(b2, ones, rsq[:, c_, :], start=(c_ == 0), stop=(c_ == nchunks - 1), mark=(c_ == nchunks - 1))
            mean, rstd, t0, t1 = stat
            ts(mean, b1, 1.0 / width, None, ALU.mult)
            tt(t0, mean, mean, ALU.mult)
            stt(t1, b2, 1.0 / width, t0, ALU.mult, ALU.subtract)
            ts(t1, t1, LN_EPS, None, ALU.add)
            act(t0, t1, AF.Sqrt)
            recip(rstd, t0)
            if not stats_only:
                ln_apply(src_chunks, nchunks, g_off, b_off, outs)

        def ln_apply(src_chunks, nchunks, g_off, b_off, outs):
            mean, rstd, t0, t1 = stat
            for c_ in range(nchunks):
                tt(src_chunks[c_], src_chunks[c_], mean, ALU.subtract)
                tt(src_chunks[c_], src_chunks[c_], rstd, ALU.mult)
                for o_tv, fn in outs(c_):
                    act(o_tv, src_chunks[c_], fn, bias=vcol(b_off + c_), scale=vcol(g_off + c_))

        S.dma("sp", vecs, dram(vecs_d[:, :]), S.stream("vecs"))
        S.dma("pool", ident, dram(ident_d[:, :]), st_tabE)
        memset(ones, 1.0)
        zpad = aview(off["pTm"][0], CC * 16, [CC, 16])
        memset(zpad, 0.0)
        S.dma("sp", TV(UT_d[:, :, 0:16].rearrange("c p t -> p c t"), B_Upad), zpad, st_misc)
        S.dma("sp", TV(UT_d[:, :, 16 + T:32 + T].rearrange("c p t -> p c t"), B_Upad), zpad, st_misc)

        out_toks = []
        pending_ln2 = []
        pending_ln2b = []
        cgen_next = [None]
        for l in range(L):
            x_in = xT_d if l == 0 else x1T_d
            x_out = yT_d if l == L - 1 else x1T_d
            in_bufs = (lambda t: []) if l == 0 else (lambda t: [B_x1[t]])
            w_in_l = w_in_d[l]
            vb = lambda base, per: base + l * per

            wctx.update(l=l, phase=None, t=0, idx=0)
            src = tab_d[l, cfg_entries[main_cfg][0][1]:cfg_entries[main_cfg][0][1] + ncfgI].rearrange(
                "e h q k -> q (e h) k")
            S.dma("pool", tabI, dram(src), st_tabI)
            memTb = aview(span_lo, KD * c.M, [KD, c.M])
            memTb.bufs = list(memTb.bufs) + extra_span
            S.dma("pool", memTb, dram(memT_d.rearrange("c p t -> p c t")), st_mem)
            for (c0, ncols) in col_groups(0, c.MW):
                def ev_k(allb, c0=c0):
                    for fc, bk in enumerate(allb[0]):
                        hm = (c0 // 128) + fc
                        act(memK[:, hm, :], bk[:, 0:c.M], AF.Copy)
                proj_group([dict(w=w_mkv_d[l], K=KD, c0=c0, ncols=ncols, rhs=lambda k: memTb[:, k, :], n=c.M)],
                           lambda allb, f=ev_k: f(allb))
            for (c0, ncols) in col_groups(c.MW, c.MW):
                def ev_v(bks, c0=c0, ncols=ncols):
                    for j, bk in enumerate(bks):
                        act(memV[:, j, c0 - c.MW:c0 - c.MW + ncols], bk[:, 0:ncols], AF.Copy)
                proj_tm(w_mkv_d[l], KD, c0, ncols, lambda k, j: memTb[:, k, j * 128:(j + 1) * 128], c.M // 128, ev_v)

            xTb2 = aview(off["KTw"][0], KD * TT, [KD, TT])
            xTbP = [xTb, xTb2]
            st_xP = [st_xTb, st_xTb2]

            def load_xTb(buf, st_, tt_):
                S.dma("pool", buf, TV(x_in[:, :, tt_ * TT:(tt_ + 1) * TT].rearrange("c p t -> p c t"), in_bufs(tt_)), st_)

            def load_windows(tt_):
                m_lo = max(0, 4 * tt_ - 2)
                m_hi = min(c.NCH, 4 * tt_ + 6)
                wlo = m_lo - (4 * tt_ - 2)
                nm_ = m_hi - m_lo
                tl = sorted({m // 4 for m in range(m_lo, m_hi)})
                S.dma("sp", KTw.v(KTw.ap[:, :, wlo * 128:(wlo + nm_) * 128]),
                      TV(KT_d[:, :, m_lo * 128:m_hi * 128].rearrange("c p t -> p c t"), [B_KT[x] for x in tl]), st_kw)
                S.dma("sp", Vw.v(Vw.ap[:, wlo:wlo + nm_, :]),
                      TV(V_d[m_lo:m_hi].rearrange("j p f -> p j f"), [B_V[x] for x in tl]), st_vw)
                ul = sorted({x for x in (tt_ - 1, tt_, tt_ + 1) if 0 <= x < NT})
                S.dma("sp", Uw, TV(UT_d[:, :, tt_ * TT:tt_ * TT + 544].rearrange("c p t -> p c t"),
                                   [B_U[x] for x in ul] + [B_Upad]), st_uw)

            load_xTb(xTbP[0], st_xP[0], 0)
            for t in range(NT):
                S.new_epoch()
                wctx.update(l=l, phase="pre", t=t, idx=0)
                tok0 = t * TT
                xTbc = xTbP[t % 2]
                xk = lambda k, b_=xTbc: b_[:, k, :]
                if True:
                    vst4 = vst.v(vst.ap[:, :, :, 64:65])
                    memset(vst4, 1.0)
                for (c0, ncols) in col_groups(c.ok, c.NAW):
                    def ev(allb, c0=c0):
                        for fc, bk in enumerate(allb[0]):
                            ch = (c0 - c.ok) // 128 + fc
                            act(kst[:, ch, :], bk, AF.Copy)
                    proj_group([dict(w=w_in_l, K=KD, c0=c0, ncols=ncols, rhs=xk)], ev)
                S.dma("sp", TV(KT_d[:, :, tok0:tok0 + TT].rearrange("c p t -> p c t"), [B_KT[t]]), kst, st_ks)
                if t + 1 < NT:
                    load_xTb(xTbP[(t + 1) % 2], st_xP[(t + 1) % 2], t + 1)
                elif NT % 2 == 0:
                    load_xTb(xTb, st_xTb, 0)
                for (ca, ncols) in col_groups(0, c.CW, 256):
                    def ev(allb, ca=ca):
                        for fc in range(len(allb[0])):
                            ch = ca // 128 + fc
                            s_ = sg[ch % 2]
                            act(s_, allb[1][fc], AF.Sigmoid)
                            tt(ust[:, ch, :], s_, allb[0][fc], ALU.mult)
                    proj_group([dict(w=w_in_l, K=KD, c0=c.oa + ca, ncols=ncols, rhs=xk),
                                dict(w=w_in_l, K=KD, c0=c.og + ca, ncols=ncols, rhs=xk)], ev)
                S.dma("sp", TV(UT_d[:, :, 16 + tok0:16 + tok0 + TT].rearrange("c p t -> p c t"), [B_U[t]]), ust, st_us)
                for (c0, ncols) in col_groups(c.ov, c.NAW):
                    def ev(bks, c0=c0, ncols=ncols):
                        h0 = (c0 - c.ov) // 64
                        nh = ncols // 64
                        for j, bk in enumerate(bks):
                            act(vst.v(vst.ap[:, j, h0:h0 + nh, 0:64]),
                                bk.v(bk.ap[:, 0:ncols].rearrange("p (h d) -> p h d", d=64)), AF.Copy)
                    proj_tm(w_in_l, KD, c0, ncols, lambda k, j, b_=xTbc: b_[:, k, j * 128:(j + 1) * 128], 4, ev)
                S.dma("sp", TV(V_d[4 * t:4 * t + 4].rearrange("j p f -> p j f"), [B_V[t]]),
                      vst.v(vst.ap.rearrange("p j h d -> p j (h d)")), st_vs)

            for t in range(NT):
                S.new_epoch()
                wctx.update(l=l, phase="main", t=t, idx=0)
                tok0 = t * TT
                xk = lambda k: xTbc_[k]
                if t == 0:
                    if NT % 2 != 0:
                        load_xTb(xTb, st_xTb, 0)
                    load_windows(0)
                cvr = [TV(cvr_t[:, ch, :], B_cvr[ch]) for ch in range(CC)]
                cvb = [TV(xres_t[:, CC + ch, :].bitcast(BF16)[:, 0:TT], B_xres[CC + ch]) for ch in range(CC)]
                cvsq = [TV(xres_t[:, CC + ch, :].bitcast(BF16)[:, TT:2 * TT], B_xres[CC + ch]) for ch in range(CC)]
                cwb = vb(c.v_cw, CC * 31)

                def conv_taps(cvr=cvr, cwb=cwb, l=l):
                    for j in range(31):
                        for ch in range(CC):
                            wj = vcol(cwb + ch * 31 + j)
                            ce = "pool" if (ch == CC - 1 and POOL_CONV) else "dve"
                            if j == 0:
                                ts(cvr[ch], Uw[:, ch, 1:1 + TT], wj, vcol(vb(c.v_cb, CC) + ch), ALU.mult, ALU.add, eng=ce)
                            else:
                                stt(cvr[ch], Uw[:, ch, j + 1:j + 1 + TT], wj, cvr[ch], ALU.mult, ALU.add, eng=ce)
                        yield j
                if t == 0 or cgen_next[0] is None:
                    cgen = conv_taps()
                else:
                    cgen = cgen_next[0]
                cgen_next[0] = None

                def taps(n):
                    for _ in range(n):
                        next(cgen, None)

                memset(qpad, 0.0)
                for (c0, ncols) in col_groups(c.oq, c.NAW):
                    def ev(allb, c0=c0):
                        for fc, bk in enumerate(allb[0]):
                            ch = (c0 - c.oq) // 128 + fc
                            act(qpad.v(qpad.ap[0:64, 2 * ch, :]), bk.v(bk.ap[0:64, :]), AF.Copy, scale=0.125)
                            act(qpad.v(qpad.ap[64:128, 2 * ch + 1, :]), bk.v(bk.ap[64:128, :]), AF.Copy, scale=0.125)
                    proj_group([dict(w=w_in_l, K=KD, c0=c0, ncols=ncols, rhs=xk)], ev)

                while pending_ln2:
                    pending_ln2.pop(0)()
                taps(8)

                na_last_b = [None]
                pairs = []
                for pi in range(4):
                    i = 4 * t + pi
                    cf = pair_cfg[i]
                    pairs.append(dict(pi=pi, i=i, cf=cf, ents=cfg_entries[cf], nchk=len(cfg_entries[cf]),
                                      qs=slice(pi * 128, (pi + 1) * 128),
                                      pvb=[banks[4 + 2 * (pi % 2)], banks[5 + 2 * (pi % 2)]]))

                def pair_setup(P):
                    ents = P["ents"]
                    if P["cf"] == main_cfg:
                        P["tab"] = tabI
                        P["ebase"] = ents[0][1]
                    else:
                        ebase = ents[0][1]
                        ne = len(ents)
                        src = tab_d[l, ebase:ebase + ne].rearrange("e h q k -> q (e h) k")
                        S.dma("pool", tabE.v(tabE.ap[:, 0:ne * H, :]), dram(src), st_tabE)
                        P["tab"] = tabE
                        P["ebase"] = ebase

                def s_phase(P, h, par):
                    ents, nchk, i, qs, tab, ebase = P["ents"], P["nchk"], P["i"], P["qs"], P["tab"], P["ebase"]
                    sb_ = [banks[par * 2], banks[par * 2 + 1]]
                    for ci, (d, e) in enumerate(ents):
                        m = i + d
                        wch = m - (4 * t - 2)
                        o_ = sb_[ci // 4][:, (ci % 4) * 128:(ci % 4 + 1) * 128]
                        lastb = (ci == nchk - 1) or (ci % 4 == 3)
                        mm(o_, KTw[:, h // 2, wch * 128:(wch + 1) * 128], qpad[:, h, qs], True, False, False)
                        mm(o_, tab[:, (e - ebase) * H + h, :], ident, False, True, lastb)
                    p_ = pT[par]
                    n0 = min(nchk, 4)
                    act(p_[:, 0:n0 * 128], sb_[0][:, 0:n0 * 128], AF.Exp)
                    if nchk > 4:
                        act(p_[:, 512:nchk * 128], sb_[1][:, 0:(nchk - 4) * 128], AF.Exp)

                def pv_phase(P, h, par):
                    ents, nchk, i, pvb = P["ents"], P["nchk"], P["i"], P["pvb"]
                    p_ = pT[par]
                    hb = pvb[h // 6] if H > 6 else pvb[0]
                    col = (h % 6) * 65
                    for ci, (d, e) in enumerate(ents):
                        m = i + d
                        wch = m - (4 * t - 2)
                        mm(hb[:, col:col + 65], p_[:, ci * 128:(ci + 1) * 128], Vw[:, wch, h * 65:(h + 1) * 65],
                           ci == 0, ci == nchk - 1, ci == nchk - 1)

                def epi_recip(P):
                    pvb = P["pvb"]
                    for g_ in range((H + 5) // 6):
                        nh = min(6, H - 6 * g_)
                        pv3 = pvb[g_].v(pvb[g_].ap[:, 0:nh * 65].rearrange("p (h d) -> p h d", d=65))
                        recip(rec.v(rec.ap[:, g_ * 8:g_ * 8 + nh].rearrange("p (h o) -> p h o", o=1)),
                              pv3.v(pv3.ap[:, :, 64:65]))

                def epi_norm(P, h):
                    g_, hh = h // 6, h % 6
                    ts(naTok[:, h * 64:(h + 1) * 64], P["pvb"][g_][:, hh * 65:hh * 65 + 64],
                       rec[:, g_ * 8 + hh:g_ * 8 + hh + 1], None, ALU.mult)

                def epi_b(P):
                    tb = P["pvb"][0]
                    tbb = tb.v(tb.ap.bitcast(BF16))
                    for ch in range(NAC):
                        o_ap, i_ap, id_ap = (tbb.ap[:, ch * 128:(ch + 1) * 128],
                                             naTok.ap[:, ch * 128:(ch + 1) * 128], ident.ap)
                        S.op("pe", lambda e, o_ap=o_ap, i_ap=i_ap, id_ap=id_ap: e.transpose(o_ap, i_ap, id_ap),
                             reads=[naTok, ident], writes=[tb], mark=(ch == NAC - 1))
                    act(naT.v(naT.ap[:, :, P["qs"]]),
                        tbb.v(tbb.ap[:, 0:NAC * 128].rearrange("p (c q) -> p c q", q=128)), AF.Copy)

                h_b = 8 if H >= 10 else H - 1

                def epi_step(P, h):
                    if h == 0:
                        epi_recip(P)
                    for j in range(H):
                        if min(H - 1, 1 + j // 2) == h:
                            epi_norm(P, j)
                    if h == h_b:
                        epi_b(P)

                items = [(P, h) for P in pairs for h in range(H)]
                pair_setup(items[0][0])
                s_phase(items[0][0], items[0][1], 0)
                for n, (P, h) in enumerate(items):
                    if n + 1 < len(items):
                        P2, h2 = items[n + 1]
                        if h2 == 0:
                            pair_setup(P2)
                        s_phase(P2, h2, (n + 1) % 2)
                    pv_phase(P, h, n % 2)
                    if P["pi"] > 0:
                        epi_step(pairs[P["pi"] - 1], h)
                    if pending_ln2b:
                        if h == 0:
                            P["ln2g"] = pending_ln2b[0][0](P["pi"])
                        for k, grp in enumerate(P["ln2g"]):
                            if min(H - 1, 1 + 2 * k) == h:
                                for f in grp:
                                    f()
                    if h == H - 1:
                        if P["pi"] == 3:
                            epi_recip(P)
                            for j in range(H):
                                epi_norm(P, j)
                            na_last_b[0] = (lambda P=P: epi_b(P))
                        taps(4)
                        if pending_ln2b and P["pi"] == 3:
                            pending_ln2b[0][1]()
                            pending_ln2b.pop(0)


                for (c0, ncols) in col_groups(c.oqm, c.MW):
                    def ev(allb, c0=c0):
                        for fc, bk in enumerate(allb[0]):
                            hm = (c0 - c.oqm) // 128 + fc
                            act(qmT[:, hm, :], bk, AF.Copy, scale=float(128 ** -0.5))
                    proj_group([dict(w=w_in_l, K=KD, c0=c0, ncols=ncols, rhs=xk)], ev)
                    if na_last_b[0] is not None:
                        na_last_b[0]()
                        na_last_b[0] = None
                assert na_last_b[0] is None
                NMC = c.M // 128
                for hm in range(MH):
                    sbs = [next_bank() for _ in range(NMC)]
                    for mc in range(NMC):
                        mm(sbs[mc], memK[:, hm, mc * 128:(mc + 1) * 128], qmT[:, hm, :], True, True, True)
                    for mc in range(NMC):
                        act(pTm[:, mc, :], sbs[mc], AF.Exp)
                    ob = next_bank()
                    db = next_bank()
                    for mc in range(NMC):
                        mm(ob, memV[:, mc, hm * 128:(hm + 1) * 128], pTm[:, mc, :], mc == 0, mc == NMC - 1, mc == NMC - 1)
                    for mc in range(NMC):
                        mm(db, ones, pTm[:, mc, :], mc == 0, mc == NMC - 1, mc == NMC - 1)
                    recip(stat[2], db)
                    tt(moT[:, hm, :], ob, stat[2], ALU.mult)
                    taps(1)

                taps(31)
                for ch in range(CC):
                    act(cvb[ch], cvr[ch], AF.Copy)
                    act(cvsq[ch], cvr[ch], AF.Square)
                b1 = next_bank()
                b2 = next_bank()
                for ch in range(CC):
                    mm(b1, ones, cvb[ch], ch == 0, ch == CC - 1, ch == CC - 1)
                for ch in range(CC):
                    mm(b2, ones, cvsq[ch], ch == 0, ch == CC - 1, ch == CC - 1)
                mean, rstd, t0, t1 = stat
                ts(mean, b1, 1.0 / c.CW, None, ALU.mult)
                tt(t0, mean, mean, ALU.mult)
                stt(t1, b2, 1.0 / c.CW, t0, ALU.mult, ALU.subtract)
                ts(t1, t1, LN_EPS, None, ALU.add)
                act(t0, t1, AF.Sqrt)
                recip(rstd, t0)
                for ch in range(CC):
                    tt(cvr[ch], cvr[ch], mean, ALU.subtract)
                    tt(cvr[ch], cvr[ch], rstd, ALU.mult)
                    act(cvT[:, ch, :], cvr[ch], AF.Silu, bias=vcol(vb(c.v_cbb, CC) + ch),
                        scale=vcol(vb(c.v_cg, CC) + ch))
                S.dma("sp", xres_all, TV(x_in[:, :, tok0:tok0 + TT].rearrange("c p t -> p c t"), in_bufs(t)), st_xres)

                branches = [(w_pa_d[l], NAC, lambda k: naT[:, k, :], c.ogn),
                            (w_pb_d[l], CC, lambda k: cvT[:, k, :], c.ogc),
                            (w_pc_d[l], MH, lambda k: moT[:, k, :], c.ogm)]
                for (f0, ncols) in col_groups(0, D):
                    for bi, (wp, Kb, rfn, og_) in enumerate(branches):
                        for (s0, ns) in col_groups(f0, ncols, 256):
                            def ev(allb, bi=bi, s0=s0, f0=f0):
                                for fc in range(len(allb[0])):
                                    fa = s0 // 128 + fc
                                    fl = fa - f0 // 128
                                    s_ = sg[fa % 2]
                                    act(s_, allb[1][fc], AF.Sigmoid)
                                    if bi == 0:
                                        tt(mixacc[:, fl, :], s_, allb[0][fc], ALU.mult)
                                    elif bi == 1:
                                        tt(s_, s_, allb[0][fc], ALU.mult)
                                        tt(mixacc[:, fl, :], mixacc[:, fl, :], s_, ALU.add)
                                    else:
                                        tt(s_, s_, allb[0][fc], ALU.mult)
                                        tt(mixT[:, fa, :], mixacc[:, fl, :], s_, ALU.add)
                            proj_group([dict(w=wp, K=Kb, c0=s0, ncols=ns, rhs=rfn, kp=16),
                                        dict(w=w_in_l, K=KD, c0=og_ + s0, ncols=ns, rhs=xk)], ev)

                for (f0, ncols) in col_groups(0, D):
                    def ev(allb, f0=f0):
                        for fc, bk in enumerate(allb[0]):
                            fa = f0 // 128 + fc
                            stt(xres[fa], xres[fa], c.alpha, bk, ALU.mult, ALU.add)
                    proj_group([dict(w=w_o_d[l], K=KD, c0=f0, ncols=ncols, rhs=lambda k: mixT[:, k, :])], ev)
                if t + 1 < NT:
                    load_windows(t + 1)
                    cgen_next[0] = conv_taps()
                layer_norm_fm(xres, KD, D, vb(c.v_l1g, KD), vb(c.v_l1b, KD),
                              lambda c_: [(xTbc_[c_], AF.Identity), (xres[c_], AF.Identity)])

                for (h0, ncols) in col_groups(0, c.DFF, 256):
                    def ev(allb, h0=h0):
                        for fc in range(len(allb[0])):
                            hc = h0 // 128 + fc
                            s_ = sg[hc % 2]
                            act(s_, allb[0][fc], AF.Silu)
                            tt(hT[:, hc, :], s_, allb[1][fc], ALU.mult)
                    proj_group([dict(w=w_fi_d[l], K=KD, c0=h0, ncols=ncols, rhs=xk),
                                dict(w=w_fi_d[l], K=KD, c0=c.DFF + h0, ncols=ncols, rhs=xk)], ev)
                    if cgen_next[0] is not None:
                        next(cgen_next[0], None)
                for (f0, ncols) in col_groups(0, D):
                    def ev(allb, f0=f0):
                        for fc, bk in enumerate(allb[0]):
                            fa = f0 // 128 + fc
                            stt(xres[fa], xres[fa], c.alpha, bk, ALU.mult, ALU.add)
                    proj_group([dict(w=w_fo_d[l], K=HC, c0=f0, ncols=ncols, rhs=lambda k: hT[:, k, :], kp=8)], ev)
                    if t + 1 < NT:
                        qi = f0 // 512
                        nq = max(1, D // 512)
                        c_lo = qi * KD // nq
                        c_hi = (qi + 1) * KD // nq
                        if c_hi > c_lo:
                            S.dma("pool", TV(xTb_t[:, c_lo:c_hi, :], B_xTb[c_lo:c_hi]),
                                  TV(x_in[c_lo:c_hi, :, (t + 1) * TT:(t + 2) * TT].rearrange("c p t -> p c t"), in_bufs(t + 1)),
                                  st_xq[qi % 4])
                    if cgen_next[0] is not None:
                        for _ in range(3):
                            next(cgen_next[0], None)
                ln_part1(xres, KD)

                def ln2_a(l=l):
                    layer_norm_fm(xres, KD, D, c.v_l2g + l * KD, c.v_l2b + l * KD,
                                  lambda c_: [(xres[c_], AF.Identity)], skip_part1=True, stats_only=True)

                def ln2_groups(q, l=l):
                    mean, rstd = stat[0], stat[1]
                    g_off, b_off = c.v_l2g + l * KD, c.v_l2b + l * KD

                    def dv(cc):
                        tt(xres[cc], xres[cc], mean, ALU.subtract)
                        tt(xres[cc], xres[cc], rstd, ALU.mult)

                    def ac(cc):
                        act(xres[cc], xres[cc], AF.Identity, bias=vcol(b_off + cc), scale=vcol(g_off + cc))
                    cs = list(range(q * KD // 4, (q + 1) * KD // 4))
                    groups = []
                    for k, cc in enumerate(cs):
                        fns = [lambda cc=cc: dv(cc)]
                        if k > 0:
                            fns.append(lambda pc=cs[k - 1]: ac(pc))
                        groups.append(fns)
                    if cs:
                        groups.append([lambda pc=cs[-1]: ac(pc)])
                    return groups

                def ln2_out(t=t, tok0=tok0, l=l, x_out=x_out):
                    ob_ = [] if l == L - 1 else [B_x1[t]]
                    tk = S.dma("sp", TV(x_out[:, :, tok0:tok0 + TT].rearrange("c p t -> p c t"), ob_), xres_all,
                               st_out)
                    out_toks.append(tk)
                if t + 1 < NT:
                    pending_ln2.append(ln2_a)
                    pending_ln2b.append((ln2_groups, ln2_out))
                else:
                    ln2_a()
                    for q in range(4):
                        for grp in ln2_groups(q):
                            for f in grp:
                                f()
                    ln2_out()

        S.wait_only("sp", out_toks[-1:])
        block = es.enter_context(nc.Block())

        @block.sync
        def _(e):
            S.replay("sp", e)

        @block.gpsimd
        def _(e):
            S.replay("pool", e)

        @block.scalar
        def _(e):
            S.replay("act", e)

        @block.vector
        def _(e):
            S.replay("dve", e)

        @block.tensor
        def _(e):
            S.replay("pe", e)
    return nc


def prep_shared(cfg, w, meta):
    c = cfg
    pair_cfg, cfg_entries, entries, main_cfg = meta
    L = c.L
    idx = na_index(entries)
    rpb = np.asarray(w["rpb"], dtype=np.float32).reshape(L, c.H, 15 * 31)
    rpb_pad = np.concatenate([rpb, np.full((L, c.H, 1), NEG, np.float32)], axis=-1)
    natab = np.ascontiguousarray(rpb_pad[:, :, idx].transpose(0, 2, 1, 3, 4))
    vecs = np.zeros((128, c.NV), np.float32)

    def put(off_, arr, nchunk):
        a = np.asarray(arr, np.float32).reshape(L, nchunk, 128).transpose(2, 0, 1).reshape(128, L * nchunk)
        vecs[:, off_:off_ + L * nchunk] = a

    cw = np.asarray(w["conv_w"], np.float32)
    cwp = cw.reshape(L, 31, c.CC, 128).transpose(3, 0, 2, 1).reshape(128, L * c.CC * 31)
    vecs[:, c.v_cw:c.v_cw + L * c.CC * 31] = cwp
    put(c.v_cb, w["conv_b"], c.CC)
    put(c.v_cg, w["conv_ln_g"], c.CC)
    put(c.v_cbb, w["conv_ln_b"], c.CC)
    put(c.v_l1g, w["ln1_g"], c.KD)
    put(c.v_l1b, w["ln1_b"], c.KD)
    put(c.v_l2g, w["ln2_g"], c.KD)
    put(c.v_l2b, w["ln2_b"], c.KD)
    sh = {"natab": natab, "vecs": vecs, "ident": np.eye(128, dtype=np.float32)}
    for k in ("w_in", "w_mem_kv", "w_pa", "w_pb", "w_pc", "w_o", "w_ffn_in", "w_ffn_out"):
        sh[k] = np.ascontiguousarray(np.asarray(w[k], np.float32))
    return sh


def run_cfg(cfg, xs, mems, w, n_cores=8):
    meta = na_meta(cfg.ROWS)
    nc = build(cfg, *meta)
    sh = prep_shared(cfg, w, meta)
    in_maps = []
    idle = None
    for ci in range(n_cores):
        if ci < len(xs):
            m = dict(sh)
            m["xT"] = np.ascontiguousarray(np.asarray(xs[ci], np.float32).T.reshape(cfg.KD, 128, cfg.T))
            m["memT"] = np.ascontiguousarray(np.asarray(mems[ci], np.float32).T.reshape(cfg.KD, 128, cfg.M))
        else:
            if idle is None:
                idle = {k: np.zeros_like(v) for k, v in sh.items()}
                idle["xT"] = np.zeros((cfg.KD, 128, cfg.T), np.float32)
                idle["memT"] = np.zeros((cfg.KD, 128, cfg.M), np.float32)
            m = dict(idle)
        in_maps.append(m)
    res = run_bass_kernel_spmd(nc, in_maps, core_ids=list(range(n_cores)))
    outs = []
    for si in range(len(xs)):
        yT = np.asarray(res.results[si]["yT"]).reshape(cfg.D, cfg.T)
        outs.append(np.ascontiguousarray(yT.T).astype(np.float32))
    return outs


def kernel(x_prompt, x_sample, mem_prompt, mem_sample, w_in, w_mem_kv, rpb, conv_w, conv_b,
           conv_ln_g, conv_ln_b, w_pa, w_pb, w_pc, w_o, ln1_g, ln1_b, w_ffn_in, w_ffn_out,
           ln2_g, ln2_b):
    cfg = Cfg()
    x_prompt = np.asarray(x_prompt)
    x_sample = np.asarray(x_sample)
    mem_prompt = np.asarray(mem_prompt)
    mem_sample = np.asarray(mem_sample)
    xs = [x_prompt[b] for b in range(x_prompt.shape[0])] + [x_sample[b] for b in range(x_sample.shape[0])]
    mems = [mem_prompt[b] for b in range(mem_prompt.shape[0])] + [mem_sample[b] for b in range(mem_sample.shape[0])]
    w = dict(w_in=w_in, w_mem_kv=w_mem_kv, rpb=rpb, conv_w=conv_w, conv_b=conv_b, conv_ln_g=conv_ln_g,
             conv_ln_b=conv_ln_b, w_pa=w_pa, w_pb=w_pb, w_pc=w_pc, w_o=w_o, ln1_g=ln1_g, ln1_b=ln1_b,
             w_ffn_in=w_ffn_in, w_ffn_out=w_ffn_out, ln2_g=ln2_g, ln2_b=ln2_b)
    outs = run_cfg(cfg, xs, mems, w)
    nb = x_prompt.shape[0]
    y_prompt = np.stack(outs[:nb], axis=0).astype(np.float32)
    y_sample = np.stack(outs[nb:], axis=0).astype(np.float32)
    return (y_prompt, y_sample)
```

```python
from contextlib import ExitStack
import numpy as np
import concourse.bass as bass
import concourse.mybir as mybir
from concourse.bass_utils import run_bass_kernel_spmd

F32 = mybir.dt.float32
BF16 = mybir.dt.bfloat16
AF = mybir.ActivationFunctionType
ALU = mybir.AluOpType
NEG = -30000.0
LN_EPS = 1e-5
TT = 512
POOL_CONV = False
GW = 64


class Cfg:
    def __init__(self, D=2048, T=4096, H=12, CW=768, MH=4, M=256, DFF=5632, L=2):
        self.D, self.T, self.H, self.CW, self.MH, self.M, self.DFF, self.L = D, T, H, CW, MH, M, DFF, L
        self.KD = D // 128
        self.NAW = H * 64
        self.NAC = self.NAW // 128
        self.CC = CW // 128
        self.MW = MH * 128
        self.HC = DFF // 128
        self.ROWS = T // GW
        self.NT = T // TT
        self.NCH = T // 128
        self.oq = 0
        self.ok = self.NAW
        self.ov = 2 * self.NAW
        self.oa = 3 * self.NAW
        self.og = 3 * self.NAW + CW
        self.oqm = 3 * self.NAW + 2 * CW
        self.ogn = self.oqm + self.MW
        self.ogc = self.ogn + D
        self.ogm = self.ogc + D
        self.IN_TOTAL = self.ogm + D
        self.alpha = float((2 * L) ** 0.25)
        o = 0
        self.v_cw = o; o += L * self.CC * 31
        self.v_cb = o; o += L * self.CC
        self.v_cg = o; o += L * self.CC
        self.v_cbb = o; o += L * self.CC
        self.v_l1g = o; o += L * self.KD
        self.v_l1b = o; o += L * self.KD
        self.v_l2g = o; o += L * self.KD
        self.v_l2b = o; o += L * self.KD
        self.NV = o


def na_meta(ROWS):
    npairs = ROWS // 2

    def start_r(r):
        return min(max(r - 4, 0), ROWS - 8)

    cfg_keys = {}
    pair_cfg = []
    entries = []
    cfg_entries = []
    for i in range(npairs):
        ms = sorted({kr // 2 for qr in (0, 1) for kr in range(start_r(2 * i + qr), start_r(2 * i + qr) + 8)})
        key = []
        for m in ms:
            pat = tuple(
                tuple(start_r(2 * i + qr) <= 2 * m + kr < start_r(2 * i + qr) + 8 for kr in (0, 1)) for qr in (0, 1))
            key.append((m - i, pat))
        key = tuple(key)
        if key not in cfg_keys:
            cfg_keys[key] = len(cfg_entries)
            lst = []
            for (d, pat) in key:
                lst.append((d, len(entries)))
                entries.append((d, pat))
            cfg_entries.append(lst)
        pair_cfg.append(cfg_keys[key])
    counts = [pair_cfg.count(c) for c in range(len(cfg_entries))]
    main_cfg = int(np.argmax(counts))
    return pair_cfg, cfg_entries, entries, main_cfg


def na_index(entries):
    E = len(entries)
    idx = np.full((E, 128, 128), 465, dtype=np.int64)
    qc = np.arange(64)[:, None]
    kc = np.arange(64)[None, :]
    sc = np.clip(qc - 8, 0, 48)
    vc = (kc >= sc) & (kc < sc + 16)
    coff = kc - qc + 15
    for e, (d, pat) in enumerate(entries):
        for qr in (0, 1):
            for kr in (0, 1):
                if not pat[qr][kr]:
                    continue
                dr = 2 * d + kr - qr
                blk = np.where(vc, (dr + 7) * 31 + coff, 465)
                idx[e, qr * 64:(qr + 1) * 64, kr * 64:(kr + 1) * 64] = blk
    return idx


class Buf:
    __slots__ = ("name", "w", "w_eng", "r")

    def __init__(self, name):
        self.name = name
        self.w = None
        self.w_eng = None
        self.r = {}


class TV:
    __slots__ = ("ap", "bufs")

    def __init__(self, ap, bufs):
        self.ap = ap
        self.bufs = bufs if isinstance(bufs, (list, tuple)) else [bufs]

    def __getitem__(self, key):
        return TV(self.ap[key], self.bufs)

    def v(self, ap):
        return TV(ap, self.bufs)


ENGS = ("pe", "act", "dve", "pool", "sp")


class Sched:
    def __init__(self, nc, es):
        self.nc = nc
        self.es = es
        self.q = {e: [] for e in ENGS}
        self.sem = {}
        self.cnt = {}
        self.nsem = 0
        self.waited = {e: {} for e in ENGS}
        self.pending_unmarked = {e: False for e in ENGS}
        for e in ("pe", "act", "dve", "pool"):
            self._new_sem(e)

    def alloc_sem(self, name):
        self.nsem += 1
        return self.es.enter_context(self.nc.semaphore(f"{name}_{self.nsem}"))

    def _new_sem(self, e):
        self.sem[e] = self.alloc_sem("c" + e)
        self.cnt[e] = 0

    def new_epoch(self):
        for e in ("pe", "act", "dve", "pool"):
            assert not self.pending_unmarked[e], e
            if self.cnt[e] > 36000:
                self._new_sem(e)

    def _waits(self, eng, reads, writes):
        ws = []
        for t in reads:
            for b in t.bufs:
                if b.w is not None:
                    ws.append(b.w)
        for t in writes:
            for b in t.bufs:
                for (e2, tok) in b.r.values():
                    if e2 != eng or eng != "pe":
                        ws.append(tok)
                if b.w is not None and (b.w_eng != eng or eng != "pe"):
                    ws.append(b.w)
        out = []
        wd = self.waited[eng]
        for (sem, val) in ws:
            k = id(sem)
            if wd.get(k, (None, 0))[1] >= val:
                continue
            wd[k] = (sem, val)
            out.append((sem, val))
        best = {}
        for (sem, val) in out:
            k = id(sem)
            if k not in best or best[k][1] < val:
                best[k] = (sem, val)
        return list(best.values())

    def _update(self, eng, tok, reads, writes):
        for t in reads:
            for b in t.bufs:
                b.r[eng if eng != "dma" else ("dma", id(tok[0]))] = (eng, tok)
        for t in writes:
            for b in t.bufs:
                b.w = tok
                b.w_eng = eng
                b.r = {}

    def op(self, eng, fn, reads=(), writes=(), mark=True):
        ws = self._waits(eng, reads, writes)
        if mark:
            self.cnt[eng] += 1
            tok = (self.sem[eng], self.cnt[eng])
            self.pending_unmarked[eng] = False
            inc = (self.sem[eng], 1)
        else:
            tok = (self.sem[eng], self.cnt[eng] + 1)
            self.pending_unmarked[eng] = True
            inc = None
        self.q[eng].append((ws, fn, inc))
        self._update(eng, tok, reads, writes)
        return tok

    def dma(self, qeng, out, in_, sem_holder, out_tracked=True, in_tracked=True):
        reads = [in_] if in_tracked else []
        writes = [out] if out_tracked else []
        ws = self._waits(qeng, reads, writes)
        if sem_holder["cnt"] > 60000:
            sem_holder["sem"] = self.alloc_sem("dr")
            sem_holder["cnt"] = 0
        sem_holder["cnt"] += 16
        tok = (sem_holder["sem"], sem_holder["cnt"])
        o_ap, i_ap = out.ap, in_.ap
        self.q[qeng].append((ws, (lambda e, o=o_ap, i=i_ap: e.dma_start(out=o, in_=i)), (sem_holder["sem"], 16)))
        self._update("dma", tok, reads, writes)
        return tok

    def stream(self, name):
        return {"sem": self.alloc_sem("d" + name), "cnt": 0}

    def wait_only(self, eng, toks):
        ws = []
        wd = self.waited[eng]
        for (sem, val) in toks:
            k = id(sem)
            if wd.get(k, (None, 0))[1] >= val:
                continue
            wd[k] = (sem, val)
            ws.append((sem, val))
        if ws:
            self.q[eng].append((ws, None, None))

    def replay(self, eng_name, e):
        for (ws, fn, inc) in self.q[eng_name]:
            for (sem, val) in ws:
                e.wait_ge(sem, val)
            if fn is None:
                continue
            ins = fn(e)
            if inc is not None:
                ins.then_inc(inc[0], inc[1])


def build(cfg, pair_cfg, cfg_entries, entries, main_cfg):
    c = cfg
    D, T, H, KD, NAC, CC, MH, HC, L, NT = c.D, c.T, c.H, c.KD, c.NAC, c.CC, c.MH, c.HC, c.L, c.NT
    E = len(entries)
    nc = bass.Bass("TRN2", target_bir_lowering=False)

    def din(name, shape):
        return nc.dram_tensor(name, list(shape), F32, kind="ExternalInput").ap()

    xT_d = din("xT", [KD, 128, T])
    memT_d = din("memT", [KD, 128, c.M])
    w_in_d = din("w_in", [L, D, c.IN_TOTAL])
    w_mkv_d = din("w_mem_kv", [L, D, 2 * c.MW])
    w_pa_d = din("w_pa", [L, c.NAW, D])
    w_pb_d = din("w_pb", [L, c.CW, D])
    w_pc_d = din("w_pc", [L, c.MW, D])
    w_o_d = din("w_o", [L, D, D])
    w_fi_d = din("w_ffn_in", [L, D, 2 * c.DFF])
    w_fo_d = din("w_ffn_out", [L, c.DFF, D])
    tab_d = din("natab", [L, E, H, 128, 128])
    vecs_d = din("vecs", [128, c.NV])
    ident_d = din("ident", [128, 128])
    yT_d = nc.dram_tensor("yT", [KD, 128, T], F32, kind="ExternalOutput").ap()
    x1T_d = nc.dram_tensor("x1T", [KD, 128, T], F32, kind="Internal").ap()
    KT_d = nc.dram_tensor("KTs", [NAC, 128, T], BF16, kind="Internal").ap()
    V_d = nc.dram_tensor("Vs", [c.NCH, 128, H * 65], BF16, kind="Internal").ap()
    UT_d = nc.dram_tensor("UTs", [CC, 128, T + 32], BF16, kind="Internal").ap()
    WTOT = (D * c.IN_TOTAL + (c.NAW + c.CW + c.MW) * D + D * D + D * 2 * c.DFF + c.DFF * D) // 128
    wscr_d = [nc.dram_tensor(f"wscr{l_}", [128, WTOT], BF16, kind="Internal").ap() for l_ in range(L)]

    es = ExitStack()
    with es:
        S = Sched(nc, es)

        def sb(name, shape, dt):
            return es.enter_context(nc.sbuf_tensor("s_" + name, list(shape), dt))

        xres_t = sb("xres", [128, KD, TT], F32)
        xTb_t = sb("xTb", [128, KD, TT], BF16)
        NSLOT = 4
        SLOT = 8 * 512
        ring_t = [sb(f"ring{i}", [128, SLOT], BF16) for i in range(NSLOT)]
        ncfgI = len(cfg_entries[main_cfg])
        tabI_t = sb("tabI", [128, ncfgI * H, 128], BF16)
        memK_t = sb("memK", [128, MH, c.M], BF16)
        memV_t = sb("memV", [128, c.M // 128, c.MW], BF16)
        vecs_t = sb("vecs", [128, c.NV], F32)
        ident_t = sb("ident", [128, 128], BF16)
        ones_t = sb("ones", [128, 128], BF16)
        stat_t = sb("stat", [128, 4, TT], F32)
        cvr_t = sb("cvr", [128, CC, TT], F32)
        B_cvr = [Buf(f"cvr{i}") for i in range(CC)]
        rec_t = sb("rec", [128, 16], F32)

        A_KTw = NAC * 1024
        A_Vw = 8 * H * 65
        A_Uw = max(CC * 544, KD * TT + 8 * TT - A_KTw - A_Vw, 2 * CC * TT - A_KTw - A_Vw)
        A_q = H * TT
        A_qm = MH * TT
        A_na = NAC * TT
        A_cv = CC * TT
        A_mo = MH * TT
        maxE = max([len(v) for i_, v in enumerate(cfg_entries) if i_ != main_cfg] + [1])
        A_tabE = maxE * H * 128
        A_pT = 2 * 768
        A_naTok = c.NAW
        A_pTm = (c.M // 128) * TT
        A_sg = 2 * 2 * TT
        off = {}
        o = 0
        for nm, sz in (("KTw", A_KTw), ("Vw", A_Vw), ("Uw", A_Uw), ("q", A_q), ("qm", A_qm), ("mo", A_mo),
                       ("cv", A_cv), ("na", A_na), ("pT", A_pT), ("naTok", A_naTok), ("pTm", A_pTm)):
            o = (o + 1) & ~1
            off[nm] = (o, sz)
            o += sz
        span_lo = off["q"][0]
        A_h = HC * TT
        A_ln = 2 * KD * TT
        need = max(o, span_lo + A_h, span_lo + A_q + A_ln, span_lo + (NAC + CC) * TT + 4 * H * 65)
        assert off["qm"][1] + off["mo"][1] + off["cv"][1] >= A_tabE, "tabE alias"
        assert A_KTw >= 2 * CC * TT or True
        arena_t = sb("arena", [128, need], BF16)
        cells = {nm: Buf(nm) for nm in off}
        tail_buf = Buf("tail")
        cell_order = list(off.keys())

        def bufs_for(lo, hi):
            bl = []
            for nm in cell_order:
                a, sz = off[nm]
                if a < hi and a + sz > lo:
                    bl.append(cells[nm])
            if hi > o:
                bl.append(tail_buf)
            return bl

        def aview(lo, n, shape=None, dt=BF16):
            ap = arena_t[:, lo:lo + n]
            if dt == F32:
                ap = ap.bitcast(F32)
            if shape is not None:
                names = " ".join(f"a{i}" for i in range(len(shape)))
                kw = {f"a{i}": s for i, s in enumerate(shape)}
                ap = ap.rearrange(f"p ({names}) -> p {names}", **kw)
            return TV(ap, bufs_for(lo, lo + n))

        KTw = aview(off["KTw"][0], A_KTw, [NAC, 1024])
        Vw = aview(off["Vw"][0], A_Vw, [8, H * 65])
        Uw = aview(off["Uw"][0], CC * 544, [CC, 544])
        qpad = aview(off["q"][0], A_q, [H, TT])
        qmT = aview(off["qm"][0], A_qm, [MH, TT])
        moT = aview(off["mo"][0], A_mo, [MH, TT])
        cvT = aview(off["cv"][0], A_cv, [CC, TT])
        naT = aview(off["na"][0], A_na, [NAC, TT])
        pT = [aview(off["pT"][0] + i * 768, 768) for i in range(2)]
        pT[0].bufs = [Buf("pT0")]
        pT[1].bufs = [Buf("pT1")]
        pT_cells = bufs_for(off["pT"][0], off["pT"][0] + A_pT)
        naTok = aview(off["naTok"][0], A_naTok)
        pTm = aview(off["pTm"][0], A_pTm, [c.M // 128, TT])
        sg_t = sb("sgt", [128, 2, TT], F32)
        sg = [TV(sg_t[:, i, :], Buf(f"sg{i}")) for i in range(2)]
        tabE = aview(off["qm"][0], A_tabE, [maxE * H, 128])
        mixT = aview(off["KTw"][0], KD * TT, [KD, TT])
        mixacc = aview(off["KTw"][0] + KD * TT, 4 * 2 * TT, [4, TT], F32)
        assert KD * TT + 8 * TT <= A_KTw + A_Vw + A_Uw
        cvb = aview(off["KTw"][0], CC * TT, [CC, TT])
        cvsq = aview(off["KTw"][0] + CC * TT, CC * TT, [CC, TT])
        assert 2 * CC * TT <= A_KTw + A_Vw
        cvr = aview(off["q"][0], 2 * CC * TT, [CC, TT], F32)
        assert 2 * CC * TT <= A_q + A_qm + A_mo or 2 * CC * TT <= A_q
        hT = aview(span_lo, A_h, [HC, TT])
        rb = aview(span_lo + A_q, KD * TT, [KD, TT])
        rsq = aview(span_lo + A_q + KD * TT, KD * TT, [KD, TT])
        kst = aview(span_lo, NAC * TT, [NAC, TT])
        ust = aview(span_lo + NAC * TT, CC * TT, [CC, TT])
        vst = aview(span_lo + (NAC + CC) * TT, 4 * H * 65, [4, H, 65])
        extra_span = [pT[0].bufs[0], pT[1].bufs[0]]
        for tv in (hT, rb, rsq, kst, ust, vst):
            tv.bufs = list(tv.bufs) + extra_span

        banks = []
        for i in range(8):
            t_ = es.enter_context(nc.psum_tensor(f"ps{i}", [128, 512], F32))
            banks.append(TV(t_[:, :], Buf(f"bank{i}")))
        bank_rr = [0]

        def next_bank():
            b = banks[bank_rr[0] % 8]
            bank_rr[0] += 1
            return b

        B_xres = [Buf(f"xres{i}") for i in range(KD)]
        xres = [TV(xres_t[:, i, :], B_xres[i]) for i in range(KD)]
        xres_all = TV(xres_t[:, :, :], B_xres)
        B_xTb = [Buf(f"xTb{i}") for i in range(KD)]
        xTb = TV(xTb_t[:, :, :], B_xTb)
        xTbc_ = [TV(xTb_t[:, i, :], B_xTb[i]) for i in range(KD)]
        ring = [TV(ring_t[i][:, :], Buf(f"ring{i}")) for i in range(NSLOT)]
        ring_st = [S.stream(f"ring{i}") for i in range(NSLOT)]
        ring_i = [0]
        wb_st = [S.stream(f"wb{i}") for i in range(NSLOT)]
        wcache = {}
        wcache_off = [0] * L
        wctx = {"l": 0, "phase": None, "t": 0, "idx": 0}
        tabI = TV(tabI_t[:, :, :], Buf("tabI"))
        memK = TV(memK_t[:, :, :], Buf("memK"))
        memV = TV(memV_t[:, :, :], Buf("memV"))
        vecs = TV(vecs_t[:, :], Buf("vecs"))
        ident = TV(ident_t[:, :], Buf("ident"))
        ones = TV(ones_t[:, :], Buf("ones"))
        stat = [TV(stat_t[:, i, :], Buf(f"stat{i}")) for i in range(4)]
        rec = TV(rec_t[:, :], Buf("rec"))

        B_x1 = [Buf(f"x1d{t}") for t in range(NT)]
        B_KT = [Buf(f"ktd{t}") for t in range(NT)]
        B_V = [Buf(f"vd{t}") for t in range(NT)]
        B_U = [Buf(f"ud{t}") for t in range(NT)]
        B_Upad = Buf("upad")

        st_misc = S.stream("misc")
        st_xres = S.stream("xres")
        st_xTb = S.stream("xTb")
        st_xTb2 = S.stream("xTb2")
        st_xq = [S.stream(f"xq{i}") for i in range(4)]
        st_kw = S.stream("kw")
        st_vw = S.stream("vw")
        st_uw = S.stream("uw")
        st_ks = S.stream("ks")
        st_us = S.stream("us")
        st_vs = S.stream("vs")
        st_out = S.stream("out")
        st_tabI = S.stream("tabI")
        st_tabE = S.stream("tabE")
        st_mem = S.stream("mem")

        def mm(out, lhsT, rhs, start, stop, mark):
            o_, l_, r_ = out.ap, lhsT.ap, rhs.ap
            return S.op("pe", lambda e: e.matmul(o_, l_, r_, start=start, stop=stop),
                        reads=[lhsT, rhs], writes=[out], mark=mark)

        def act(out, in_, func, bias=None, scale=None, extra_reads=()):
            o_, i_ = out.ap, in_.ap
            kw = {}
            if bias is not None:
                kw["bias"] = bias.ap if isinstance(bias, TV) else bias
            if scale is not None:
                kw["scale"] = scale.ap if isinstance(scale, TV) else scale
            rd = [in_] + [x for x in (bias, scale) if isinstance(x, TV)] + list(extra_reads)
            return S.op("act", lambda e: e.activation(o_, i_, func, **kw), reads=rd, writes=[out])

        def tt(out, in0, in1, op):
            o_, a_, b_ = out.ap, in0.ap, in1.ap
            return S.op("dve", lambda e: e.tensor_tensor(o_, a_, b_, op), reads=[in0, in1], writes=[out])

        def ts(out, in0, s1, s2, op0, op1=None, eng="dve"):
            o_, a_ = out.ap, in0.ap
            s1_ = s1.ap if isinstance(s1, TV) else s1
            s2_ = s2.ap if isinstance(s2, TV) else s2
            rd = [in0] + [x for x in (s1, s2) if isinstance(x, TV)]
            if op1 is None:
                return S.op(eng, lambda e: e.tensor_scalar(o_, a_, s1_, None, op0), reads=rd, writes=[out])
            return S.op(eng, lambda e: e.tensor_scalar(o_, a_, s1_, s2_, op0, op1), reads=rd, writes=[out])

        def stt(out, in0, scalar, in1, op0, op1, eng="dve"):
            o_, a_, b_ = out.ap, in0.ap, in1.ap
            s_ = scalar.ap if isinstance(scalar, TV) else scalar
            rd = [in0, in1] + ([scalar] if isinstance(scalar, TV) else [])
            return S.op(eng, lambda e: e.scalar_tensor_tensor(o_, a_, s_, b_, op0, op1), reads=rd, writes=[out])

        def recip(out, in_):
            o_, i_ = out.ap, in_.ap
            return S.op("dve", lambda e: e.reciprocal(o_, i_), reads=[in_], writes=[out])

        def memset(out, val, eng="dve"):
            o_ = out.ap
            return S.op(eng, lambda e: e.memset(o_, val), reads=[], writes=[out])

        def dram(ap):
            return TV(ap, [])

        def wpiece(w2d, r0, nk, c0, ncols):
            s = ring_i[0] % NSLOT
            ring_i[0] += 1
            assert nk * ncols <= SLOT
            n = nk * ncols
            flat = ring[s].v(ring[s].ap[:, 0:n])
            dst = ring[s].v(ring[s].ap[:, 0:n].rearrange("p (k n) -> p k n", k=nk))
            src = w2d[r0:r0 + nk * 128, c0:c0 + ncols].rearrange("(k p) n -> p k n", p=128)
            if wctx["phase"] is None:
                S.dma("pool", dst, dram(src), ring_st[s])
                return dst
            l_ = wctx["l"]
            key = (l_, wctx["phase"], wctx["idx"])
            wctx["idx"] += 1
            sig = (r0, nk, c0, ncols)
            if wctx["t"] == 0:
                o_ = wcache_off[l_]
                wcache_off[l_] += n
                assert wcache_off[l_] <= WTOT
                cb = Buf("wc")
                wcache[key] = (o_, sig, cb)
                S.dma("pool", dst, dram(src), ring_st[s])
                S.dma("sp", TV(wscr_d[l_][:, o_:o_ + n], [cb]), flat, wb_st[s])
            else:
                o_, sig0, cb = wcache[key]
                assert sig0 == sig, (key, sig0, sig)
                S.dma("pool", flat, TV(wscr_d[l_][:, o_:o_ + n], [cb]), ring_st[s])
            return dst

        def split_k(K, kp):
            out_ = []
            k0 = 0
            while k0 < K:
                n = min(kp, K - k0)
                out_.append((k0, n))
                k0 += n
            return out_

        def proj_group(specs, evac):
            allb = []
            for sp in specs:
                w2d, K, c0, ncols, rhs_fn = sp["w"], sp["K"], sp["c0"], sp["ncols"], sp["rhs"]
                kp = sp.get("kp", 8)
                r0 = sp.get("r0", 0)
                nn = sp.get("n", TT)
                nfc = ncols // 128
                bks = [next_bank() for _ in range(nfc)]
                pieces = split_k(K, kp)
                for gi, (k0, nk) in enumerate(pieces):
                    pc = wpiece(w2d, r0 + k0 * 128, nk, c0, ncols)
                    for fc in range(nfc):
                        for k in range(nk):
                            last = (gi == len(pieces) - 1 and k == nk - 1)
                            mm(bks[fc][:, 0:nn], pc[:, k, fc * 128:(fc + 1) * 128], rhs_fn(k0 + k),
                               start=(gi == 0 and k == 0), stop=last,
                               mark=last or (fc == nfc - 1 and k == nk - 1))
                allb.append(bks)
            evac(allb)

        def proj_tm(w2d, K, c0, ncols, lhs_fn, njt, evac, kp=8):
            bks = [next_bank() for _ in range(njt)]
            pieces = split_k(K, kp)
            for gi, (k0, nk) in enumerate(pieces):
                pc = wpiece(w2d, k0 * 128, nk, c0, ncols)
                for j in range(njt):
                    for k in range(nk):
                        last = (gi == len(pieces) - 1 and k == nk - 1)
                        mm(bks[j][:, 0:ncols], lhs_fn(k0 + k, j), pc[:, k, :],
                           start=(gi == 0 and k == 0), stop=last, mark=last or (j == njt - 1 and k == nk - 1))
            evac(bks)

        def col_groups(c0, n, g=512):
            out_ = []
            a = 0
            while a < n:
                m = min(g, n - a)
                out_.append((c0 + a, m))
                a += m
            return out_

        def vcol(o_, n=1):
            return vecs[:, o_:o_ + n]

        xk = lambda k: xTb[:, k, :]

        def ln_part1(src_chunks, nchunks):
            for c_ in range(nchunks):
                act(rb[:, c_, :], src_chunks[c_], AF.Copy)
                act(rsq[:, c_, :], src_chunks[c_], AF.Square)

        def layer_norm_fm(src_chunks, nchunks, width, g_off, b_off, outs, skip_part1=False, stats_only=False):
            if not skip_part1:
                ln_part1(src_chunks, nchunks)
            b1 = next_bank()
            b2 = next_bank()
            for c_ in range(nchunks):
                mm(b1, ones, rb[:, c_, :], start=(c_ == 0), stop=(c_ == nchunks - 1), mark=(c_ == nchunks - 1))
            for c_ in range(nchunks):
                mm(b2, ones, rsq[:, c_, :], start=(c_ == 0), stop=(c_ == nchunks - 1), mark=(c_ == nchunks - 1))
            mean, rstd, t0, t1 = stat
            ts(mean, b1, 1.0 / width, None, ALU.mult)
            tt(t0, mean, mean, ALU.mult)
            stt(t1, b2, 1.0 / width, t0, ALU.mult, ALU.subtract)
            ts(t1, t1, LN_EPS, None, ALU.add)
            act(t0, t1, AF.Sqrt)
            recip(rstd, t0)
            if not stats_only:
                ln_apply(src_chunks, nchunks, g_off, b_off, outs)

        def ln_apply(src_chunks, nchunks, g_off, b_off, outs):
            mean, rstd, t0, t1 = stat
            for c_ in range(nchunks):
                tt(src_chunks[c_], src_chunks[c_], mean, ALU.subtract)
                tt(src_chunks[c_], src_chunks[c_], rstd, ALU.mult)
                for o_tv, fn in outs(c_):
                    act(o_tv, src_chunks[c_], fn, bias=vcol(b_off + c_), scale=vcol(g_off + c_))

        S.dma("sp", vecs, dram(vecs_d[:, :]), S.stream("vecs"))
        S.dma("pool", ident, dram(ident_d[:, :]), st_tabE)
        memset(ones, 1.0)
        zpad = aview(off["pTm"][0], CC * 16, [CC, 16])
        memset(zpad, 0.0)
        S.dma("sp", TV(UT_d[:, :, 0:16].rearrange("c p t -> p c t"), B_Upad), zpad, st_misc)
        S.dma("sp", TV(UT_d[:, :, 16 + T:32 + T].rearrange("c p t -> p c t"), B_Upad), zpad, st_misc)

        out_toks = []
        pending_ln2 = []
        pending_ln2b = []
        cgen_next = [None]
        for l in range(L):
            x_in = xT_d if l == 0 else x1T_d
            x_out = yT_d if l == L - 1 else x1T_d
            in_bufs = (lambda t: []) if l == 0 else (lambda t: [B_x1[t]])
            w_in_l = w_in_d[l]
            vb = lambda base, per: base + l * per

            wctx.update(l=l, phase=None, t=0, idx=0)
            src = tab_d[l, cfg_entries[main_cfg][0][1]:cfg_entries[main_cfg][0][1] + ncfgI].rearrange(
                "e h q k -> q (e h) k")
            S.dma("pool", tabI, dram(src), st_tabI)
            memTb = aview(span_lo, KD * c.M, [KD, c.M])
            memTb.bufs = list(memTb.bufs) + extra_span
            S.dma("pool", memTb, dram(memT_d.rearrange("c p t -> p c t")), st_mem)
            for (c0, ncols) in col_groups(0, c.MW):
                def ev_k(allb, c0=c0):
                    for fc, bk in enumerate(allb[0]):
                        hm = (c0 // 128) + fc
                        act(memK[:, hm, :], bk[:, 0:c.M], AF.Copy)
                proj_group([dict(w=w_mkv_d[l], K=KD, c0=c0, ncols=ncols, rhs=lambda k: memTb[:, k, :], n=c.M)],
                           lambda allb, f=ev_k: f(allb))
            for (c0, ncols) in col_groups(c.MW, c.MW):
                def ev_v(bks, c0=c0, ncols=ncols):
                    for j, bk in enumerate(bks):
                        act(memV[:, j, c0 - c.MW:c0 - c.MW + ncols], bk[:, 0:ncols], AF.Copy)
                proj_tm(w_mkv_d[l], KD, c0, ncols, lambda k, j: memTb[:, k, j * 128:(j + 1) * 128], c.M // 128, ev_v)

            xTb2 = aview(off["KTw"][0], KD * TT, [KD, TT])
            xTbP = [xTb, xTb2]
            st_xP = [st_xTb, st_xTb2]

            def load_xTb(buf, st_, tt_):
                S.dma("pool", buf, TV(x_in[:, :, tt_ * TT:(tt_ + 1) * TT].rearrange("c p t -> p c t"), in_bufs(tt_)), st_)

            def load_windows(tt_):
                m_lo = max(0, 4 * tt_ - 2)
                m_hi = min(c.NCH, 4 * tt_ + 6)
                wlo = m_lo - (4 * tt_ - 2)
                nm_ = m_hi - m_lo
                tl = sorted({m // 4 for m in range(m_lo, m_hi)})
                S.dma("sp", KTw.v(KTw.ap[:, :, wlo * 128:(wlo + nm_) * 128]),
                      TV(KT_d[:, :, m_lo * 128:m_hi * 128].rearrange("c p t -> p c t"), [B_KT[x] for x in tl]), st_kw)
                S.dma("sp", Vw.v(Vw.ap[:, wlo:wlo + nm_, :]),
                      TV(V_d[m_lo:m_hi].rearrange("j p f -> p j f"), [B_V[x] for x in tl]), st_vw)
                ul = sorted({x for x in (tt_ - 1, tt_, tt_ + 1) if 0 <= x < NT})
                S.dma("sp", Uw, TV(UT_d[:, :, tt_ * TT:tt_ * TT + 544].rearrange("c p t -> p c t"),
                                   [B_U[x] for x in ul] + [B_Upad]), st_uw)

            load_xTb(xTbP[0], st_xP[0], 0)
            for t in range(NT):
                S.new_epoch()
                wctx.update(l=l, phase="pre", t=t, idx=0)
                tok0 = t * TT
                xTbc = xTbP[t % 2]
                xk = lambda k, b_=xTbc: b_[:, k, :]
                if True:
                    vst4 = vst.v(vst.ap[:, :, :, 64:65])
                    memset(vst4, 1.0)
                for (c0, ncols) in col_groups(c.ok, c.NAW):
                    def ev(allb, c0=c0):
                        for fc, bk in enumerate(allb[0]):
                            ch = (c0 - c.ok) // 128 + fc
                            act(kst[:, ch, :], bk, AF.Copy)
                    proj_group([dict(w=w_in_l, K=KD, c0=c0, ncols=ncols, rhs=xk)], ev)
                S.dma("sp", TV(KT_d[:, :, tok0:tok0 + TT].rearrange("c p t -> p c t"), [B_KT[t]]), kst, st_ks)
                if t + 1 < NT:
                    load_xTb(xTbP[(t + 1) % 2], st_xP[(t + 1) % 2], t + 1)
                elif NT % 2 == 0:
                    load_xTb(xTb, st_xTb, 0)
                for (ca, ncols) in col_groups(0, c.CW, 256):
                    def ev(allb, ca=ca):
                        for fc in range(len(allb[0])):
                            ch = ca // 128 + fc
                            s_ = sg[ch % 2]
                            act(s_, allb[1][fc], AF.Sigmoid)
                            tt(ust[:, ch, :], s_, allb[0][fc], ALU.mult)
                    proj_group([dict(w=w_in_l, K=KD, c0=c.oa + ca, ncols=ncols, rhs=xk),
                                dict(w=w_in_l, K=KD, c0=c.og + ca, ncols=ncols, rhs=xk)], ev)
                S.dma("sp", TV(UT_d[:, :, 16 + tok0:16 + tok0 + TT].rearrange("c p t -> p c t"), [B_U[t]]), ust, st_us)
                for (c0, ncols) in col_groups(c.ov, c.NAW):
                    def ev(bks, c0=c0, ncols=ncols):
                        h0 = (c0 - c.ov) // 64
                        nh = ncols // 64
                        for j, bk in enumerate(bks):
                            act(vst.v(vst.ap[:, j, h0:h0 + nh, 0:64]),
                                bk.v(bk.ap[:, 0:ncols].rearrange("p (h d) -> p h d", d=64)), AF.Copy)
                    proj_tm(w_in_l, KD, c0, ncols, lambda k, j, b_=xTbc: b_[:, k, j * 128:(j + 1) * 128], 4, ev)
                S.dma("sp", TV(V_d[4 * t:4 * t + 4].rearrange("j p f -> p j f"), [B_V[t]]),
                      vst.v(vst.ap.rearrange("p j h d -> p j (h d)")), st_vs)

            for t in range(NT):
                S.new_epoch()
                wctx.update(l=l, phase="main", t=t, idx=0)
                tok0 = t * TT
                xk = lambda k: xTbc_[k]
                if t == 0:
                    if NT % 2 != 0:
                        load_xTb(xTb, st_xTb, 0)
                    load_windows(0)
                cvr = [TV(cvr_t[:, ch, :], B_cvr[ch]) for ch in range(CC)]
                cvb = [TV(xres_t[:, CC + ch, :].bitcast(BF16)[:, 0:TT], B_xres[CC + ch]) for ch in range(CC)]
                cvsq = [TV(xres_t[:, CC + ch, :].bitcast(BF16)[:, TT:2 * TT], B_xres[CC + ch]) for ch in range(CC)]
                cwb = vb(c.v_cw, CC * 31)

                def conv_taps(cvr=cvr, cwb=cwb, l=l):
                    for j in range(31):
                        for ch in range(CC):
                            wj = vcol(cwb + ch * 31 + j)
                            ce = "pool" if (ch == CC - 1 and POOL_CONV) else "dve"
                            if j == 0:
                                ts(cvr[ch], Uw[:, ch, 1:1 + TT], wj, vcol(vb(c.v_cb, CC) + ch), ALU.mult, ALU.add, eng=ce)
                            else:
                                stt(cvr[ch], Uw[:, ch, j + 1:j + 1 + TT], wj, cvr[ch], ALU.mult, ALU.add, eng=ce)
                        yield j
                if t == 0 or cgen_next[0] is None:
                    cgen = conv_taps()
                else:
                    cgen = cgen_next[0]
                cgen_next[0] = None

                def taps(n):
                    for _ in range(n):
                        next(cgen, None)

                memset(qpad, 0.0)
                for (c0, ncols) in col_groups(c.oq, c.NAW):
                    def ev(allb, c0=c0):
                        for fc, bk in enumerate(allb[0]):
                            ch = (c0 - c.oq) // 128 + fc
                            act(qpad.v(qpad.ap[0:64, 2 * ch, :]), bk.v(bk.ap[0:64, :]), AF.Copy, scale=0.125)
                            act(qpad.v(qpad.ap[64:128, 2 * ch + 1, :]), bk.v(bk.ap[64:128, :]), AF.Copy, scale=0.125)
                    proj_group([dict(w=w_in_l, K=KD, c0=c0, ncols=ncols, rhs=xk)], ev)

                while pending_ln2:
                    pending_ln2.pop(0)()
                taps(8)

                na_last_b = [None]
                pairs = []
                for pi in range(4):
                    i = 4 * t + pi
                    cf = pair_cfg[i]
                    pairs.append(dict(pi=pi, i=i, cf=cf, ents=cfg_entries[cf], nchk=len(cfg_entries[cf]),
                                      qs=slice(pi * 128, (pi + 1) * 128),
                                      pvb=[banks[4 + 2 * (pi % 2)], banks[5 + 2 * (pi % 2)]]))

                def pair_setup(P):
                    ents = P["ents"]
                    if P["cf"] == main_cfg:
                        P["tab"] = tabI
                        P["ebase"] = ents[0][1]
                    else:
                        ebase = ents[0][1]
                        ne = len(ents)
                        src = tab_d[l, ebase:ebase + ne].rearrange("e h q k -> q (e h) k")
                        S.dma("pool", tabE.v(tabE.ap[:, 0:ne * H, :]), dram(src), st_tabE)
                        P["tab"] = tabE
                        P["ebase"] = ebase

                def s_phase(P, h, par):
                    ents, nchk, i, qs, tab, ebase = P["ents"], P["nchk"], P["i"], P["qs"], P["tab"], P["ebase"]
                    sb_ = [banks[par * 2], banks[par * 2 + 1]]
                    for ci, (d, e) in enumerate(ents):
                        m = i + d
                        wch = m - (4 * t - 2)
                        o_ = sb_[ci // 4][:, (ci % 4) * 128:(ci % 4 + 1) * 128]
                        lastb = (ci == nchk - 1) or (ci % 4 == 3)
                        mm(o_, KTw[:, h // 2, wch * 128:(wch + 1) * 128], qpad[:, h, qs], True, False, False)
                        mm(o_, tab[:, (e - ebase) * H + h, :], ident, False, True, lastb)
                    p_ = pT[par]
                    n0 = min(nchk, 4)
                    act(p_[:, 0:n0 * 128], sb_[0][:, 0:n0 * 128], AF.Exp)
                    if nchk > 4:
                        act(p_[:, 512:nchk * 128], sb_[1][:, 0:(nchk - 4) * 128], AF.Exp)

                def pv_phase(P, h, par):
                    ents, nchk, i, pvb = P["ents"], P["nchk"], P["i"], P["pvb"]
                    p_ = pT[par]
                    hb = pvb[h // 6] if H > 6 else pvb[0]
                    col = (h % 6) * 65
                    for ci, (d, e) in enumerate(ents):
                        m = i + d
                        wch = m - (4 * t - 2)
                        mm(hb[:, col:col + 65], p_[:, ci * 128:(ci + 1) * 128], Vw[:, wch, h * 65:(h + 1) * 65],
                           ci == 0, ci == nchk - 1, ci == nchk - 1)

                def epi_recip(P):
                    pvb = P["pvb"]
                    for g_ in range((H + 5) // 6):
                        nh = min(6, H - 6 * g_)
                        pv3 = pvb[g_].v(pvb[g_].ap[:, 0:nh * 65].rearrange("p (h d) -> p h d", d=65))
                        recip(rec.v(rec.ap[:, g_ * 8:g_ * 8 + nh].rearrange("p (h o) -> p h o", o=1)),
                              pv3.v(pv3.ap[:, :, 64:65]))

                def epi_norm(P, h):
                    g_, hh = h // 6, h % 6
                    ts(naTok[:, h * 64:(h + 1) * 64], P["pvb"][g_][:, hh * 65:hh * 65 + 64],
                       rec[:, g_ * 8 + hh:g_ * 8 + hh + 1], None, ALU.mult)

                def epi_b(P):
                    tb = P["pvb"][0]
                    tbb = tb.v(tb.ap.bitcast(BF16))
                    for ch in range(NAC):
                        o_ap, i_ap, id_ap = (tbb.ap[:, ch * 128:(ch + 1) * 128],
                                             naTok.ap[:, ch * 128:(ch + 1) * 128], ident.ap)
                        S.op("pe", lambda e, o_ap=o_ap, i_ap=i_ap, id_ap=id_ap: e.transpose(o_ap, i_ap, id_ap),
                             reads=[naTok, ident], writes=[tb], mark=(ch == NAC - 1))
                    act(naT.v(naT.ap[:, :, P["qs"]]),
                        tbb.v(tbb.ap[:, 0:NAC * 128].rearrange("p (c q) -> p c q", q=128)), AF.Copy)

                h_b = 8 if H >= 10 else H - 1

                def epi_step(P, h):
                    if h == 0:
                        epi_recip(P)
                    for j in range(H):
                        if min(H - 1, 1 + j // 2) == h:
                            epi_norm(P, j)
                    if h == h_b:
                        epi_b(P)

                items = [(P, h) for P in pairs for h in range(H)]
                pair_setup(items[0][0])
                s_phase(items[0][0], items[0][1], 0)
                for n, (P, h) in enumerate(items):
                    if n + 1 < len(items):
                        P2, h2 = items[n + 1]
                        if h2 == 0:
                            pair_setup(P2)
                        s_phase(P2, h2, (n + 1) % 2)
                    pv_phase(P, h, n % 2)
                    if P["pi"] > 0:
                        epi_step(pairs[P["pi"] - 1], h)
                    if pending_ln2b:
                        if h == 0:
                            P["ln2g"] = pending_ln2b[0][0](P["pi"])
                        for k, grp in enumerate(P["ln2g"]):
                            if min(H - 1, 1 + 2 * k) == h:
                                for f in grp:
                                    f()
                    if h == H - 1:
                        if P["pi"] == 3:
                            epi_recip(P)
                            for j in range(H):
                                epi_norm(P, j)
                            na_last_b[0] = (lambda P=P: epi_b(P))
                        taps(4)
                        if pending_ln2b and P["pi"] == 3:
                            pending_ln2b[0][1]()
                            pending_ln2b.pop(0)


                for (c0, ncols) in col_groups(c.oqm, c.MW):
                    def ev(allb, c0=c0):
                        for fc, bk in enumerate(allb[0]):
                            hm = (c0 - c.oqm) // 128 + fc
                            act(qmT[:, hm, :], bk, AF.Copy, scale=float(128 ** -0.5))
                    proj_group([dict(w=w_in_l, K=KD, c0=c0, ncols=ncols, rhs=xk)], ev)
                    if na_last_b[0] is not None:
                        na_last_b[0]()
                        na_last_b[0] = None
                assert na_last_b[0] is None
                NMC = c.M // 128
                for hm in range(MH):
                    sbs = [next_bank() for _ in range(NMC)]
                    for mc in range(NMC):
                        mm(sbs[mc], memK[:, hm, mc * 128:(mc + 1) * 128], qmT[:, hm, :], True, True, True)
                    for mc in range(NMC):
                        act(pTm[:, mc, :], sbs[mc], AF.Exp)
                    ob = next_bank()
                    db = next_bank()
                    for mc in range(NMC):
                        mm(ob, memV[:, mc, hm * 128:(hm + 1) * 128], pTm[:, mc, :], mc == 0, mc == NMC - 1, mc == NMC - 1)
                    for mc in range(NMC):
                        mm(db, ones, pTm[:, mc, :], mc == 0, mc == NMC - 1, mc == NMC - 1)
                    recip(stat[2], db)
                    tt(moT[:, hm, :], ob, stat[2], ALU.mult)
                    taps(1)

                taps(31)
                for ch in range(CC):
                    act(cvb[ch], cvr[ch], AF.Copy)
                    act(cvsq[ch], cvr[ch], AF.Square)
                b1 = next_bank()
                b2 = next_bank()
                for ch in range(CC):
                    mm(b1, ones, cvb[ch], ch == 0, ch == CC - 1, ch == CC - 1)
                for ch in range(CC):
                    mm(b2, ones, cvsq[ch], ch == 0, ch == CC - 1, ch == CC - 1)
                mean, rstd, t0, t1 = stat
                ts(mean, b1, 1.0 / c.CW, None, ALU.mult)
                tt(t0, mean, mean, ALU.mult)
                stt(t1, b2, 1.0 / c.CW, t0, ALU.mult, ALU.subtract)
                ts(t1, t1, LN_EPS, None, ALU.add)
                act(t0, t1, AF.Sqrt)
                recip(rstd, t0)
                for ch in range(CC):
                    tt(cvr[ch], cvr[ch], mean, ALU.subtract)
                    tt(cvr[ch], cvr[ch], rstd, ALU.mult)
                    act(cvT[:, ch, :], cvr[ch], AF.Silu, bias=vcol(vb(c.v_cbb, CC) + ch),
                        scale=vcol(vb(c.v_cg, CC) + ch))
                S.dma("sp", xres_all, TV(x_in[:, :, tok0:tok0 + TT].rearrange("c p t -> p c t"), in_bufs(t)), st_xres)

                branches = [(w_pa_d[l], NAC, lambda k: naT[:, k, :], c.ogn),
                            (w_pb_d[l], CC, lambda k: cvT[:, k, :], c.ogc),
                            (w_pc_d[l], MH, lambda k: moT[:, k, :], c.ogm)]
                for (f0, ncols) in col_groups(0, D):
                    for bi, (wp, Kb, rfn, og_) in enumerate(branches):
                        for (s0, ns) in col_groups(f0, ncols, 256):
                            def ev(allb, bi=bi, s0=s0, f0=f0):
                                for fc in range(len(allb[0])):
                                    fa = s0 // 128 + fc
                                    fl = fa - f0 // 128
                                    s_ = sg[fa % 2]
                                    act(s_, allb[1][fc], AF.Sigmoid)
                                    if bi == 0:
                                        tt(mixacc[:, fl, :], s_, allb[0][fc], ALU.mult)
                                    elif bi == 1:
                                        tt(s_, s_, allb[0][fc], ALU.mult)
                                        tt(mixacc[:, fl, :], mixacc[:, fl, :], s_, ALU.add)
                                    else:
                                        tt(s_, s_, allb[0][fc], ALU.mult)
                                        tt(mixT[:, fa, :], mixacc[:, fl, :], s_, ALU.add)
                            proj_group([dict(w=wp, K=Kb, c0=s0, ncols=ns, rhs=rfn, kp=16),
                                        dict(w=w_in_l, K=KD, c0=og_ + s0, ncols=ns, rhs=xk)], ev)

                for (f0, ncols) in col_groups(0, D):
                    def ev(allb, f0=f0):
                        for fc, bk in enumerate(allb[0]):
                            fa = f0 // 128 + fc
                            stt(xres[fa], xres[fa], c.alpha, bk, ALU.mult, ALU.add)
                    proj_group([dict(w=w_o_d[l], K=KD, c0=f0, ncols=ncols, rhs=lambda k: mixT[:, k, :])], ev)
                if t + 1 < NT:
                    load_windows(t + 1)
                    cgen_next[0] = conv_taps()
                layer_norm_fm(xres, KD, D, vb(c.v_l1g, KD), vb(c.v_l1b, KD),
                              lambda c_: [(xTbc_[c_], AF.Identity), (xres[c_], AF.Identity)])

                for (h0, ncols) in col_groups(0, c.DFF, 256):
                    def ev(allb, h0=h0):
                        for fc in range(len(allb[0])):
                            hc = h0 // 128 + fc
                            s_ = sg[hc % 2]
                            act(s_, allb[0][fc], AF.Silu)
                            tt(hT[:, hc, :], s_, allb[1][fc], ALU.mult)
                    proj_group([dict(w=w_fi_d[l], K=KD, c0=h0, ncols=ncols, rhs=xk),
                                dict(w=w_fi_d[l], K=KD, c0=c.DFF + h0, ncols=ncols, rhs=xk)], ev)
                    if cgen_next[0] is not None:
                        next(cgen_next[0], None)
                for (f0, ncols) in col_groups(0, D):
                    def ev(allb, f0=f0):
                        for fc, bk in enumerate(allb[0]):
                            fa = f0 // 128 + fc
                            stt(xres[fa], xres[fa], c.alpha, bk, ALU.mult, ALU.add)
                    proj_group([dict(w=w_fo_d[l], K=HC, c0=f0, ncols=ncols, rhs=lambda k: hT[:, k, :], kp=8)], ev)
                    if t + 1 < NT:
                        qi = f0 // 512
                        nq = max(1, D // 512)
                        c_lo = qi * KD // nq
                        c_hi = (qi + 1) * KD // nq
                        if c_hi > c_lo:
                            S.dma("pool", TV(xTb_t[:, c_lo:c_hi, :], B_xTb[c_lo:c_hi]),
                                  TV(x_in[c_lo:c_hi, :, (t + 1) * TT:(t + 2) * TT].rearrange("c p t -> p c t"), in_bufs(t + 1)),
                                  st_xq[qi % 4])
                    if cgen_next[0] is not None:
                        for _ in range(3):
                            next(cgen_next[0], None)
                ln_part1(xres, KD)

                def ln2_a(l=l):
                    layer_norm_fm(xres, KD, D, c.v_l2g + l * KD, c.v_l2b + l * KD,
                                  lambda c_: [(xres[c_], AF.Identity)], skip_part1=True, stats_only=True)

                def ln2_groups(q, l=l):
                    mean, rstd = stat[0], stat[1]
                    g_off, b_off = c.v_l2g + l * KD, c.v_l2b + l * KD

                    def dv(cc):
                        tt(xres[cc], xres[cc], mean, ALU.subtract)
                        tt(xres[cc], xres[cc], rstd, ALU.mult)

                    def ac(cc):
                        act(xres[cc], xres[cc], AF.Identity, bias=vcol(b_off + cc), scale=vcol(g_off + cc))
                    cs = list(range(q * KD // 4, (q + 1) * KD // 4))
                    groups = []
                    for k, cc in enumerate(cs):
                        fns = [lambda cc=cc: dv(cc)]
                        if k > 0:
                            fns.append(lambda pc=cs[k - 1]: ac(pc))
                        groups.append(fns)
                    if cs:
                        groups.append([lambda pc=cs[-1]: ac(pc)])
                    return groups

                def ln2_out(t=t, tok0=tok0, l=l, x_out=x_out):
                    ob_ = [] if l == L - 1 else [B_x1[t]]
                    tk = S.dma("sp", TV(x_out[:, :, tok0:tok0 + TT].rearrange("c p t -> p c t"), ob_), xres_all,
                               st_out)
                    out_toks.append(tk)
                if t + 1 < NT:
                    pending_ln2.append(ln2_a)
                    pending_ln2b.append((ln2_groups, ln2_out))
                else:
                    ln2_a()
                    for q in range(4):
                        for grp in ln2_groups(q):
                            for f in grp:
                                f()
                    ln2_out()

        S.wait_only("sp", out_toks[-1:])
        block = es.enter_context(nc.Block())

        @block.sync
        def _(e):
            S.replay("sp", e)

        @block.gpsimd
        def _(e):
            S.replay("pool", e)

        @block.scalar
        def _(e):
            S.replay("act", e)

        @block.vector
        def _(e):
            S.replay("dve", e)

        @block.tensor
        def _(e):
            S.replay("pe", e)
    return nc


def prep_shared(cfg, w, meta):
    c = cfg
    pair_cfg, cfg_entries, entries, main_cfg = meta
    L = c.L
    idx = na_index(entries)
    rpb = np.asarray(w["rpb"], dtype=np.float32).reshape(L, c.H, 15 * 31)
    rpb_pad = np.concatenate([rpb, np.full((L, c.H, 1), NEG, np.float32)], axis=-1)
    natab = np.ascontiguousarray(rpb_pad[:, :, idx].transpose(0, 2, 1, 3, 4))
    vecs = np.zeros((128, c.NV), np.float32)

    def put(off_, arr, nchunk):
        a = np.asarray(arr, np.float32).reshape(L, nchunk, 128).transpose(2, 0, 1).reshape(128, L * nchunk)
        vecs[:, off_:off_ + L * nchunk] = a

    cw = np.asarray(w["conv_w"], np.float32)
    cwp = cw.reshape(L, 31, c.CC, 128).transpose(3, 0, 2, 1).reshape(128, L * c.CC * 31)
    vecs[:, c.v_cw:c.v_cw + L * c.CC * 31] = cwp
    put(c.v_cb, w["conv_b"], c.CC)
    put(c.v_cg, w["conv_ln_g"], c.CC)
    put(c.v_cbb, w["conv_ln_b"], c.CC)
    put(c.v_l1g, w["ln1_g"], c.KD)
    put(c.v_l1b, w["ln1_b"], c.KD)
    put(c.v_l2g, w["ln2_g"], c.KD)
    put(c.v_l2b, w["ln2_b"], c.KD)
    sh = {"natab": natab, "vecs": vecs, "ident": np.eye(128, dtype=np.float32)}
    for k in ("w_in", "w_mem_kv", "w_pa", "w_pb", "w_pc", "w_o", "w_ffn_in", "w_ffn_out"):
        sh[k] = np.ascontiguousarray(np.asarray(w[k], np.float32))
    return sh


def run_cfg(cfg, xs, mems, w, n_cores=8):
    meta = na_meta(cfg.ROWS)
    nc = build(cfg, *meta)
    sh = prep_shared(cfg, w, meta)
    in_maps = []
    idle = None
    for ci in range(n_cores):
        if ci < len(xs):
            m = dict(sh)
            m["xT"] = np.ascontiguousarray(np.asarray(xs[ci], np.float32).T.reshape(cfg.KD, 128, cfg.T))
            m["memT"] = np.ascontiguousarray(np.asarray(mems[ci], np.float32).T.reshape(cfg.KD, 128, cfg.M))
        else:
            if idle is None:
                idle = {k: np.zeros_like(v) for k, v in sh.items()}
                idle["xT"] = np.zeros((cfg.KD, 128, cfg.T), np.float32)
                idle["memT"] = np.zeros((cfg.KD, 128, cfg.M), np.float32)
            m = dict(idle)
        in_maps.append(m)
    res = run_bass_kernel_spmd(nc, in_maps, core_ids=list(range(n_cores)))
    outs = []
    for si in range(len(xs)):
        yT = np.asarray(res.results[si]["yT"]).reshape(cfg.D, cfg.T)
        outs.append(np.ascontiguousarray(yT.T).astype(np.float32))
    return outs


def kernel(x_prompt, x_sample, mem_prompt, mem_sample, w_in, w_mem_kv, rpb, conv_w, conv_b,
           conv_ln_g, conv_ln_b, w_pa, w_pb, w_pc, w_o, ln1_g, ln1_b, w_ffn_in, w_ffn_out,
           ln2_g, ln2_b):
    cfg = Cfg()
    x_prompt = np.asarray(x_prompt)
    x_sample = np.asarray(x_sample)
    mem_prompt = np.asarray(mem_prompt)
    mem_sample = np.asarray(mem_sample)
    xs = [x_prompt[b] for b in range(x_prompt.shape[0])] + [x_sample[b] for b in range(x_sample.shape[0])]
    mems = [mem_prompt[b] for b in range(mem_prompt.shape[0])] + [mem_sample[b] for b in range(mem_sample.shape[0])]
    w = dict(w_in=w_in, w_mem_kv=w_mem_kv, rpb=rpb, conv_w=conv_w, conv_b=conv_b, conv_ln_g=conv_ln_g,
             conv_ln_b=conv_ln_b, w_pa=w_pa, w_pb=w_pb, w_pc=w_pc, w_o=w_o, ln1_g=ln1_g, ln1_b=ln1_b,
             w_ffn_in=w_ffn_in, w_ffn_out=w_ffn_out, ln2_g=ln2_g, ln2_b=ln2_b)
    outs = run_cfg(cfg, xs, mems, w)
    nb = x_prompt.shape[0]
    y_prompt = np.stack(outs[:nb], axis=0).astype(np.float32)
    y_sample = np.stack(outs[nb:], axis=0).astype(np.float32)
    return (y_prompt, y_sample)
```
